# Optimizing a Trainium2 kernel written in Bass

```python
import jax
import jax.numpy as jnp
from jax import lax
import numpy as np

D_MODEL = 1024
BATCH = 16
SEQ = 256
DEPTH = 2
DEC_BATCH = 2
DEC_SEQ = 4096
PAST_LEN = 512

GRID_W = 64
Q_BLOCK = 128
ROPE_THETA = 10000.0
EPS = 1e-6

GQA_HEADS = 8
GQA_KV_HEADS = 2
GQA_REP = GQA_HEADS // GQA_KV_HEADS
GQA_HEAD_DIM = 64
GQA_WIDTH = GQA_HEADS * GQA_HEAD_DIM
GQA_KV_WIDTH = GQA_KV_HEADS * GQA_HEAD_DIM
GLA_HEADS = 4
GLA_DK = 64
GLA_DV = 128
GLA_WIDTH = GLA_HEADS * GLA_DV
GLA_K_WIDTH = GLA_HEADS * GLA_DK
GLA_RANK = 16
GLA_NORMALIZER = 16.0
GLA_CHUNK = 64
MLA_HEADS = 4
MLA_Q_LORA = 256
MLA_KV_LORA = 256
MLA_NOPE_DIM = 64
MLA_ROPE_DIM = 32
MLA_V_DIM = 128
MLA_WIDTH = MLA_HEADS * MLA_V_DIM

IN_SPLITS = (GQA_WIDTH, GQA_KV_WIDTH, GQA_KV_WIDTH, GQA_WIDTH,
             GLA_K_WIDTH, GLA_K_WIDTH, GLA_WIDTH, GLA_WIDTH, GLA_RANK, GLA_RANK,
             MLA_Q_LORA, MLA_KV_LORA, MLA_ROPE_DIM, MLA_WIDTH,
             D_MODEL, D_MODEL, D_MODEL)
N_IN = sum(IN_SPLITS)

kernel_name = 'hybrid_diffusion_gqa_gla_mla_step'


def rms_norm(x, g):
    xf = x.astype(jnp.float32)
    y = xf * lax.rsqrt(jnp.mean(xf * xf, axis=-1, keepdims=True) + EPS)
    return (y * g.astype(jnp.float32)).astype(x.dtype)


def rope_1d(x, pos):
    half = x.shape[-1] // 2
    freqs = ROPE_THETA ** (-jnp.arange(half, dtype=jnp.float32) / half)
    ang = pos[:, None] * freqs
    ang = ang.reshape((1, ang.shape[0]) + (1,) * (x.ndim - 3) + (half,))
    cos, sin = jnp.cos(ang), jnp.sin(ang)
    xf = x.astype(jnp.float32)
    x1, x2 = xf[..., :half], xf[..., half:]
    return jnp.concatenate([x1 * cos - x2 * sin, x1 * sin + x2 * cos], axis=-1).astype(x.dtype)


def axial_rope(x, row, col):
    h = x.shape[-1] // 2
    return jnp.concatenate([rope_1d(x[..., :h], row), rope_1d(x[..., h:], col)], axis=-1)


def block_attention(q, k, v):
    b, nq, g, r, dq = q.shape
    dv = v.shape[-1]
    scale = dq ** -0.5
    qb = q.reshape(b, nq // Q_BLOCK, Q_BLOCK, g, r, dq).swapaxes(0, 1)

    def one_block(qi):
        s = jnp.einsum('bqgrd,bkgd->bgrqk', qi, k).astype(jnp.float32) * scale
        p = jax.nn.softmax(s, axis=-1).astype(v.dtype)
        return jnp.einsum('bgrqk,bkgv->bqgrv', p, v)

    o = lax.map(one_block, qb)
    return o.swapaxes(0, 1).reshape(b, nq, g * r, dv)


def gla_chunked(q, k, v, log_a, s0):
    b, n, h, dk = q.shape
    dv = v.shape[-1]
    nc = n // GLA_CHUNK

    def chunks(t):
        return t.astype(jnp.float32).reshape((b, nc, GLA_CHUNK) + t.shape[2:]).swapaxes(0, 1)

    mask = jnp.tril(jnp.ones((GLA_CHUNK, GLA_CHUNK), dtype=bool))[None, :, :, None, None]

    def step(s, inp):
        qc, kc, vc, gc = inp
        cum = jnp.cumsum(gc, axis=1)
        last = cum[:, -1]
        o_inter = jnp.einsum('bchk,bhkv->bchv', qc * jnp.exp(cum), s)
        decay = jnp.exp(jnp.where(mask, cum[:, :, None] - cum[:, None, :], -jnp.inf))
        att = jnp.einsum('bihk,bjhk,bijhk->bhij', qc, kc, decay)
        o_intra = jnp.einsum('bhij,bjhv->bihv', att, vc)
        s_new = jnp.exp(last)[..., None] * s + jnp.einsum(
            'bjhk,bjhv->bhkv', kc * jnp.exp(last[:, None] - cum), vc)
        return s_new, o_inter + o_intra

    s_fin, o = lax.scan(step, s0.astype(jnp.float32), (chunks(q), chunks(k), chunks(v), chunks(log_a)))
    return o.swapaxes(0, 1).reshape(b, n, h, dv), s_fin


def gla_bidirectional(q, k, v, la_f, la_b, s0_f, s0_b):
    flip = lambda t: jnp.flip(t, axis=1)
    o_f, s_f = gla_chunked(q, k, v, la_f, s0_f)
    o_b, s_b = gla_chunked(flip(q), flip(k), flip(v), flip(la_b), s0_b)
    return o_f + flip(o_b), s_f, s_b


def modulation(cond, w, b):
    m = (jax.nn.silu(cond) @ w + b)[..., None, :]
    return jnp.split(m, 3, axis=-1)


def mixer(h, p, pos, cache):
    bsz, n, _ = h.shape
    z = h @ p['w_in']
    offs = [int(i) for i in np.cumsum(IN_SPLITS)[:-1]]
    (qa, ka, va, ga, qg, kg, vg, gg, rf, rb,
     qlat, kvlat, kr, gc, m1, m2, m3) = jnp.split(z, offs, axis=-1)
    is_ctx = cache is None

    qa = rms_norm(qa.reshape(bsz, n, GQA_KV_HEADS, GQA_REP, GQA_HEAD_DIM), p['g_q_norm'])
    ka = rms_norm(ka.reshape(bsz, n, GQA_KV_HEADS, GQA_HEAD_DIM), p['g_k_norm'])
    va = va.reshape(bsz, n, GQA_KV_HEADS, GQA_HEAD_DIM)
    if is_ctx:
        k_all, v_all = ka, va
    else:
        row, col = pos
        qa = axial_rope(qa, row, col)
        k_all = jnp.concatenate([cache['gqa_k'].astype(h.dtype), axial_rope(ka, row, col)], axis=1)
        v_all = jnp.concatenate([cache['gqa_v'].astype(h.dtype), va], axis=1)
    y_a = block_attention(qa, k_all, v_all).reshape(bsz, n, GQA_WIDTH)
    y_a = (y_a * jax.nn.silu(ga)) @ p['w_o_gqa']

    qg = qg.reshape(bsz, n, GLA_HEADS, GLA_DK) * (GLA_DK ** -0.5)
    kg = kg.reshape(bsz, n, GLA_HEADS, GLA_DK)
    vg = vg.reshape(bsz, n, GLA_HEADS, GLA_DV)
    la_f = jax.nn.log_sigmoid((rf @ p['w_gla_decay_fwd'] + p['b_gla_decay_fwd']).astype(jnp.float32))
    la_b = jax.nn.log_sigmoid((rb @ p['w_gla_decay_bwd'] + p['b_gla_decay_bwd']).astype(jnp.float32))
    la_f = la_f.reshape(bsz, n, GLA_HEADS, GLA_DK) / GLA_NORMALIZER
    la_b = la_b.reshape(bsz, n, GLA_HEADS, GLA_DK) / GLA_NORMALIZER
    if is_ctx:
        s0_f = jnp.zeros((bsz, GLA_HEADS, GLA_DK, GLA_DV), jnp.float32)
        s0_b = s0_f
    else:
        s0_f, s0_b = cache['gla_fwd'], cache['gla_bwd']
    o_g, s_f, s_b = gla_bidirectional(qg, kg, vg, la_f, la_b, s0_f, s0_b)
    o_g = rms_norm(o_g.astype(h.dtype), p['g_gla_out']).reshape(bsz, n, GLA_WIDTH)
    y_b = (o_g * jax.nn.silu(gg)) @ p['w_o_gla']

    cq = (rms_norm(qlat, p['g_mla_q']) @ p['w_mla_uq']).reshape(bsz, n, MLA_HEADS, MLA_NOPE_DIM + MLA_ROPE_DIM)
    q_nope, q_rope = cq[..., :MLA_NOPE_DIM], cq[..., MLA_NOPE_DIM:]
    ckv = rms_norm(kvlat, p['g_mla_kv'])
    if is_ctx:
        ckv_all, kr_all = ckv, kr
    else:
        q_rope = axial_rope(q_rope, row, col)
        kr_lat = axial_rope(kr[:, :, None, :], row, col)[:, :, 0]
        ckv_all = jnp.concatenate([cache['mla_ckv'].astype(h.dtype), ckv], axis=1)
        kr_all = jnp.concatenate([cache['mla_krope'].astype(h.dtype), kr_lat], axis=1)
    nk = ckv_all.shape[1]
    kv = (ckv_all @ p['w_mla_ukv']).reshape(bsz, nk, MLA_HEADS, MLA_NOPE_DIM + MLA_V_DIM)
    k_nope, v_c = kv[..., :MLA_NOPE_DIM], kv[..., MLA_NOPE_DIM:]
    k_c = jnp.concatenate(
        [k_nope, jnp.broadcast_to(kr_all[:, :, None, :], (bsz, nk, MLA_HEADS, MLA_ROPE_DIM))], axis=-1)
    q_c = jnp.concatenate([q_nope, q_rope], axis=-1)[:, :, :, None, :]
    y_c = block_attention(q_c, k_c, v_c).reshape(bsz, n, MLA_WIDTH)
    y_c = (y_c * jax.nn.silu(gc)) @ p['w_o_mla']

    merged = jax.nn.sigmoid(m1) * y_a + jax.nn.sigmoid(m2) * y_b + jax.nn.sigmoid(m3) * y_c
    out = merged @ p['w_out']
    ctx_tensors = (ka, va, ckv, kr, s_f, s_b) if is_ctx else None
    return out, ctx_tensors


def setup_inputs(seed: int = 0) -> dict:
    key = jax.random.key(seed)
    keys = iter(jax.random.split(key, 40))
    L, D = DEPTH, D_MODEL

    def nrm(shape, scale):
        return jax.random.normal(next(keys), shape, jnp.float32) * scale

    return {
        'x_prompt': nrm((BATCH, SEQ, D), 1.0),
        'x_sample': nrm((DEC_BATCH, DEC_SEQ, D), 1.0),
        'cache_gqa_k': nrm((DEC_BATCH, DEPTH, PAST_LEN, GQA_KV_HEADS, GQA_HEAD_DIM), 1.0),
        'cache_gqa_v': nrm((DEC_BATCH, DEPTH, PAST_LEN, GQA_KV_HEADS, GQA_HEAD_DIM), 1.0),
        'cache_mla_ckv': nrm((DEC_BATCH, DEPTH, PAST_LEN, MLA_KV_LORA), 1.0),
        'cache_mla_krope': nrm((DEC_BATCH, DEPTH, PAST_LEN, MLA_ROPE_DIM), 1.0),
        'state_gla_fwd': nrm((DEC_BATCH, DEPTH, GLA_HEADS, GLA_DK, GLA_DV), 0.5),
        'state_gla_bwd': nrm((DEC_BATCH, DEPTH, GLA_HEADS, GLA_DK, GLA_DV), 0.5),
        'c': nrm((DEC_BATCH, D), 1.0),
        'c_ctx': nrm((D,), 1.0),
        'w_mod': nrm((L, D, 3 * D), 0.5 * D ** -0.5),
        'b_mod': nrm((L, 3 * D), 0.02),
        'g_pre': 1.0 + nrm((L, D), 0.02),
        'g_post': 1.0 + nrm((L, D), 0.02),
        'w_in': nrm((L, D, N_IN), D ** -0.5),
        'g_q_norm': 1.0 + nrm((L, GQA_HEAD_DIM), 0.02),
        'g_k_norm': 1.0 + nrm((L, GQA_HEAD_DIM), 0.02),
        'w_gla_decay_fwd': nrm((L, GLA_RANK, GLA_K_WIDTH), GLA_RANK ** -0.5),
        'b_gla_decay_fwd': nrm((L, GLA_K_WIDTH), 0.1),
        'w_gla_decay_bwd': nrm((L, GLA_RANK, GLA_K_WIDTH), GLA_RANK ** -0.5),
        'b_gla_decay_bwd': nrm((L, GLA_K_WIDTH), 0.1),
        'g_gla_out': 1.0 + nrm((L, GLA_DV), 0.02),
        'g_mla_q': 1.0 + nrm((L, MLA_Q_LORA), 0.02),
        'g_mla_kv': 1.0 + nrm((L, MLA_KV_LORA), 0.02),
        'w_mla_uq': nrm((L, MLA_Q_LORA, MLA_HEADS * (MLA_NOPE_DIM + MLA_ROPE_DIM)), MLA_Q_LORA ** -0.5),
        'w_mla_ukv': nrm((L, MLA_KV_LORA, MLA_HEADS * (MLA_NOPE_DIM + MLA_V_DIM)), MLA_KV_LORA ** -0.5),
        'w_o_gqa': nrm((L, GQA_WIDTH, D), GQA_WIDTH ** -0.5),
        'w_o_gla': nrm((L, GLA_WIDTH, D), GLA_WIDTH ** -0.5),
        'w_o_mla': nrm((L, MLA_WIDTH, D), MLA_WIDTH ** -0.5),
        'w_out': nrm((L, D, D), D ** -0.5),
    }


def reference(x_prompt, x_sample, cache_gqa_k, cache_gqa_v, cache_mla_ckv, cache_mla_krope,
              state_gla_fwd, state_gla_bwd, c, c_ctx, w_mod, b_mod, g_pre, g_post, w_in,
              g_q_norm, g_k_norm, w_gla_decay_fwd, b_gla_decay_fwd, w_gla_decay_bwd,
              b_gla_decay_bwd, g_gla_out, g_mla_q, g_mla_kv, w_mla_uq, w_mla_ukv,
              w_o_gqa, w_o_gla, w_o_mla, w_out):
    rows = x_sample.shape[1] // GRID_W
    row = jnp.repeat(jnp.arange(rows), GRID_W).astype(jnp.float32)
    col = jnp.tile(jnp.arange(GRID_W), rows).astype(jnp.float32)

    def layer_params(l):
        return {
            'w_in': w_in[l], 'g_q_norm': g_q_norm[l], 'g_k_norm': g_k_norm[l],
            'w_gla_decay_fwd': w_gla_decay_fwd[l], 'b_gla_decay_fwd': b_gla_decay_fwd[l],
            'w_gla_decay_bwd': w_gla_decay_bwd[l], 'b_gla_decay_bwd': b_gla_decay_bwd[l],
            'g_gla_out': g_gla_out[l], 'g_mla_q': g_mla_q[l], 'g_mla_kv': g_mla_kv[l],
            'w_mla_uq': w_mla_uq[l], 'w_mla_ukv': w_mla_ukv[l],
            'w_o_gqa': w_o_gqa[l], 'w_o_gla': w_o_gla[l], 'w_o_mla': w_o_mla[l], 'w_out': w_out[l],
        }

    def sub_layer(x, cond, l, pos, cache):
        shift, scale, gate = modulation(cond, w_mod[l], b_mod[l])
        h = rms_norm(x, g_pre[l]) * (1 + scale) + shift
        out, ctx = mixer(h, layer_params(l), pos, cache)
        return x + gate * rms_norm(out, g_post[l]), ctx

    xp = x_prompt
    ks, vs, ckvs, krs, sfs, sbs = [], [], [], [], [], []
    for l in range(DEPTH):
        xp, ctx = sub_layer(xp, c_ctx, l, None, None)
        ks.append(ctx[0]); vs.append(ctx[1]); ckvs.append(ctx[2])
        krs.append(ctx[3]); sfs.append(ctx[4]); sbs.append(ctx[5])
    y_prompt = xp

    xs = x_sample
    for l in range(DEPTH):
        cache = {
            'gqa_k': cache_gqa_k[:, l], 'gqa_v': cache_gqa_v[:, l],
            'mla_ckv': cache_mla_ckv[:, l], 'mla_krope': cache_mla_krope[:, l],
            'gla_fwd': state_gla_fwd[:, l], 'gla_bwd': state_gla_bwd[:, l],
        }
        xs, _ = sub_layer(xs, c, l, (row, col), cache)
    y_sample = xs

    return (y_prompt, y_sample, jnp.stack(ks, axis=1), jnp.stack(vs, axis=1),
            jnp.stack(ckvs, axis=1), jnp.stack(krs, axis=1),
            jnp.stack(sfs, axis=1), jnp.stack(sbs, axis=1))
```

```python
import contextlib
import numpy as np
import concourse.bass as bass
import concourse.mybir as mybir
from concourse.bass_utils import run_bass_kernel_spmd

F32 = mybir.dt.float32
BF16 = mybir.dt.bfloat16
ALU = mybir.AluOpType
AF = mybir.ActivationFunctionType

NCORES = 8
L = 2
D = 1024
NIN = 6976
TP = 512
TS = 1024
T = TP + TS
NT = T // 128
PAST = 512
NKS = PAST + 4096
EPS = 1e-6
O_QA, O_KA, O_VA, O_GA = 0, 512, 640, 768
O_QG, O_KG, O_VG, O_GG = 1280, 1536, 1792, 2304
O_RF, O_RB = 2816, 2832
O_QL, O_KV, O_KR, O_GC = 2848, 3104, 3360, 3392
O_M1, O_M2, O_M3 = 3904, 4928, 5952
BLOCKS = [(0, 512), (512, 512), (1024, 512)]
AGR = 544

CC_ASYNC = False
STAGE = 99
MLA_CUT = 99
SKIP_GQA = False
GLA_CUT = 99
FULL_CUT = 99
SKIP_MLA = False
MLA_JOBS = 4


class Emitter:
    ENGS = ("pe", "act", "dve", "pool", "sp")

    def __init__(self, n_dma_sems=20):
        self.ops = {e: [] for e in self.ENGS}
        self.cnt = {e: 0 for e in self.ENGS}
        self.sem = {}
        self.res = {}
        self.waited = {e: {} for e in self.ENGS}
        self.n_dma_sems = n_dma_sems
        self.dma_rr = 0
        self.dma_rr2 = [0, 0]
        self.all_tokens = {}

    def sems_needed(self):
        return len(self.ENGS) + self.n_dma_sems

    def bind_sems(self, sems):
        for i, e in enumerate(self.ENGS):
            self.sem[e] = sems[i]
        self.dma_sems = list(sems[len(self.ENGS):len(self.ENGS) + self.n_dma_sems])
        self.dma_cnt = [0] * len(self.dma_sems)

    def _need(self, eng, tokens):
        out = []
        for (s, v, owner) in tokens:
            if owner == eng and eng == "pe":
                continue
            w = self.waited[eng]
            if w.get(id(s), 0) >= v:
                continue
            w[id(s)] = v
            out.append((s, v))
        return out

    def _deps(self, eng, reads, writes):
        toks = []
        for r in reads:
            st = self.res.get(r)
            if not st:
                continue
            if st["w"] is not None:
                toks.append(st["w"])
            if isinstance(r, tuple) and r[0] == "PS":
                toks.extend(t for t in st["r"].values() if t[2] != eng)
        for w in writes:
            st = self.res.get(w)
            if st:
                if st["w"] is not None:
                    toks.append(st["w"])
                toks.extend(st["r"].values())
        return self._need(eng, toks)

    def _commit(self, tok, reads, writes):
        self.all_tokens[id(tok[0])] = max(self.all_tokens.get(id(tok[0]), (None, 0, None)), tok, key=lambda t: t[1])
        for r in reads:
            st = self.res.setdefault(r, {"w": None, "r": {}})
            st["r"][(tok[2], id(tok[0]))] = tok
        for w in writes:
            self.res[w] = {"w": tok, "r": {}}

    def op(self, eng, fn, reads=(), writes=(), inc=True):
        waits = self._deps(eng, reads, writes)
        sem = self.sem[eng]
        if inc:
            self.cnt[eng] += 1
            val = self.cnt[eng]
        else:
            val = self.cnt[eng] + 1
        self._commit((sem, val, eng), reads, writes)

        def run(e, waits=waits, fn=fn, inc=inc, sem=sem):
            for (s, v) in waits:
                e.wait_ge(s, v)
            ins = fn(e)
            if inc:
                ins.then_inc(sem, 1)
        self.ops[eng].append(run)

    def dma(self, eng, out, in_, reads=(), writes=(), **kw):
        half = len(self.dma_sems) // 2
        q = 1 if eng == "pool" else 0
        k = q * half + self.dma_rr2[q]
        self.dma_rr2[q] = (self.dma_rr2[q] + 1) % half
        s = self.dma_sems[k]
        prev = self.dma_cnt[k]
        self.dma_cnt[k] += 1
        toks = [(s, 16 * prev, "dma")] if prev else []
        waits = self._need(eng, toks) + self._deps(eng, reads, writes)
        self._commit((s, 16 * self.dma_cnt[k], "dma"), reads, writes)

        def run(e, waits=waits, s=s):
            for (ss, v) in waits:
                e.wait_ge(ss, v)
            e.dma_start(out=out, in_=in_, **kw).then_inc(s, 16)
        self.ops[eng].append(run)

    def collective(self, sem, fn, reads=(), writes=()):
        waits = self._deps("pool", reads, writes)
        self._commit((sem, 1, "cc"), reads, writes)

        def run(e, waits=waits):
            for (ss, v) in waits:
                e.wait_ge(ss, v)
            fn(e).then_inc(sem)
        self.ops["pool"].append(run)

    def barrier(self):
        toks = [t for t in self.all_tokens.values() if (t[2] != "cc" or not CC_ASYNC)]
        for eng in self.ENGS:
            waits = self._need(eng, toks)

            def run(e, waits=waits):
                for (s, v) in waits:
                    e.wait_ge(s, v)
            self.ops[eng].append(run)
        self.res = {k: {"w": v["w"], "r": {}} for k, v in self.res.items() if v["w"] is not None and v["w"][2] == "cc"}

    def finish(self, eng="sp"):
        toks = list(self.all_tokens.values())
        waits = self._need(eng, toks)

        def run(e, waits=waits):
            for (s, v) in waits:
                e.wait_ge(s, v)
        self.ops[eng].append(run)

    def replay(self, block):
        ops = self.ops

        @block.tensor
        def _(e):
            for f in ops["pe"]:
                f(e)

        @block.scalar
        def _(e):
            for f in ops["act"]:
                f(e)

        @block.vector
        def _(e):
            for f in ops["dve"]:
                f(e)

        @block.gpsimd
        def _(e):
            for f in ops["pool"]:
                f(e)

        @block.sync
        def _(e):
            for f in ops["sp"]:
                f(e)


class V:
    __slots__ = ("ap", "keys")

    def __init__(self, ap, keys):
        self.ap = ap
        self.keys = list(keys) if isinstance(keys, list) else [keys]


def _keys(*vs):
    out = []
    for v in vs:
        if v is None or isinstance(v, (int, float)):
            continue
        out.extend(v.keys)
    return out


def _ap(x):
    return x.ap if isinstance(x, V) else x


class Prog:
    def __init__(self):
        self.nc = bass.Bass("TRN2", target_bir_lowering=False)
        self.em = Emitter()
        self.st = contextlib.ExitStack()
        self.ps_rr = {"m": 0, "a": 0, "x": 0}
        self.cc_sems = []

    def din(self, name, shape, dt=F32):
        return self.nc.dram_tensor(name, list(shape), dt, kind="ExternalInput").ap()

    def dout(self, name, shape, dt=F32):
        return self.nc.dram_tensor(name, list(shape), dt, kind="ExternalOutput").ap()

    def dscr(self, name, shape, dt=F32):
        return self.nc.dram_tensor(name, list(shape), dt)

    def sb(self, name, shape, dt=F32):
        return self.st.enter_context(self.nc.sbuf_tensor(name, list(shape), dt))

    def mm(self, out, pairs, fp32_ok=True):
        n = len(pairs)
        for i, (lhsT, rhs) in enumerate(pairs):
            self.em.op("pe", lambda e, o=out.ap, a=lhsT.ap, b=rhs.ap, i=i: e.matmul(o, lhsT=a, rhs=b, start=(i == 0), stop=(i == n - 1)),
                       reads=_keys(lhsT, rhs), writes=_keys(out), inc=(i == n - 1))

    def mm1(self, out, lhsT, rhs, start, stop, inc=True):
        self.em.op("pe", lambda e: e.matmul(out.ap, lhsT=lhsT.ap, rhs=rhs.ap, start=start, stop=stop),
                   reads=_keys(lhsT, rhs), writes=_keys(out), inc=inc)

    def transpose(self, out, in_, ident):
        self.em.op("pe", lambda e: e.transpose(out.ap, in_.ap, ident.ap), reads=_keys(in_, ident), writes=_keys(out))

    def act(self, out, in_, func, bias=None, scale=None, accum=None, eng="act"):
        kw = {}
        if bias is not None:
            kw["bias"] = _ap(bias)
        if scale is not None:
            kw["scale"] = _ap(scale)
        if accum is not None:
            kw["accum_out"] = accum.ap
        self.em.op(eng, lambda e: e.activation(out=out.ap, in_=in_.ap, func=func, **kw),
                   reads=_keys(in_, bias, scale), writes=_keys(out, accum))

    def tt(self, out, in0, in1, op, eng="dve"):
        self.em.op(eng, lambda e: e.tensor_tensor(out=out.ap, in0=in0.ap, in1=in1.ap, op=op),
                   reads=_keys(in0, in1), writes=_keys(out))

    def ts(self, out, in0, s1, op0, s2=None, op1=None, eng="dve"):
        if op1 is None:
            self.em.op(eng, lambda e: e.tensor_scalar(out=out.ap, in0=in0.ap, scalar1=_ap(s1), scalar2=None, op0=op0),
                       reads=_keys(in0, s1), writes=_keys(out))
        else:
            self.em.op(eng, lambda e: e.tensor_scalar(out=out.ap, in0=in0.ap, scalar1=_ap(s1), scalar2=_ap(s2), op0=op0, op1=op1),
                       reads=_keys(in0, s1, s2), writes=_keys(out))

    def stt(self, out, in0, scalar, in1, op0, op1, eng="dve"):
        self.em.op(eng, lambda e: e.scalar_tensor_tensor(out=out.ap, in0=in0.ap, scalar=_ap(scalar), in1=in1.ap, op0=op0, op1=op1),
                   reads=_keys(in0, scalar, in1), writes=_keys(out))

    def copy(self, out, in_, eng="dve"):
        if eng == "act":
            self.em.op("act", lambda e: e.copy(out=out.ap, in_=in_.ap), reads=_keys(in_), writes=_keys(out))
        else:
            self.em.op(eng, lambda e: e.tensor_copy(out=out.ap, in_=in_.ap), reads=_keys(in_), writes=_keys(out))

    def recip(self, out, in_):
        self.em.op("dve", lambda e: e.reciprocal(out=out.ap, in_=in_.ap), reads=_keys(in_), writes=_keys(out))

    def memset(self, out, val, eng="pool"):
        self.em.op(eng, lambda e: e.memset(out.ap, val), writes=_keys(out))

    def dma(self, out, in_, eng="sp", **kw):
        self.em.dma(eng, out.ap, in_.ap, reads=_keys(in_), writes=_keys(out), **kw)

    def psum(self, cls):
        lo, n = {"m": (0, 4), "a": (4, 2), "x": (6, 2)}[cls]
        i = lo + self.ps_rr[cls] % n
        self.ps_rr[cls] += 1
        return i


def rope_tables(core):
    q = core % 4
    t = np.arange(q * TS, (q + 1) * TS)
    row = (t // 64).astype(np.float32)
    col = (t % 64).astype(np.float32)

    def tab(half_dims):
        fr = (np.float32(10000.0) ** (-np.arange(half_dims, dtype=np.float32) / np.float32(half_dims))).astype(np.float32)
        ar = (row[None, :] * fr[:, None]).astype(np.float32)
        ac = (col[None, :] * fr[:, None]).astype(np.float32)
        cos = np.concatenate([np.cos(ar), np.cos(ar), np.cos(ac), np.cos(ac)], 0)
        sin = np.concatenate([-np.sin(ar), np.sin(ar), -np.sin(ac), np.sin(ac)], 0)
        return cos.astype(np.float32), sin.astype(np.float32)

    c64, s64 = tab(16)
    c32, s32 = tab(8)
    return c64, s64, c32, s32


def perm_matrix(n):
    h = n // 4
    p = np.zeros((n, n), np.float32)
    for m in range(n):
        blk, r = divmod(m, 2 * h)
        sw = blk * 2 * h + (r + h) % (2 * h)
        p[sw, m] = 1.0
    return p


def build_program():
    P = Prog()
    nc, em = P.nc, P.em
    xin = P.din("xin", [T, D])
    condT = P.din("condT", [128, 8, 3])
    w_mod = P.din("w_mod", [L, D, 768])
    b_mod = P.din("b_mod", [L, 3 * D])
    g_post = P.din("g_post", [L, D])
    g_pre_c = P.din("g_pre_c", [128, L, 8])
    w_in = P.din("w_in", [L, D, NIN])
    gqk_c = P.din("gqk_c", [64, L, 2])
    wdec = P.din("wdec", [L, 2, 17, 256])
    g_gla_c = P.din("g_gla_c", [128, L])
    g_mq_c = P.din("g_mq_c", [128, L, 2])
    g_mkv_c = P.din("g_mkv_c", [128, L, 2])
    w_uq = P.din("w_uq", [L, 256, 384])
    w_ukv = P.din("w_ukv", [L, 256, 768])
    w_o = [P.din("w_o_gqa", [L, 512, D]), P.din("w_o_gla", [L, 512, D]), P.din("w_o_mla", [L, 512, D])]
    w_out = P.din("w_out", [L, D, D])
    c_k = P.din("c_k", [L, PAST, 128])
    c_v = P.din("c_v", [L, PAST, 128])
    c_ckv = P.din("c_ckv", [L, PAST, 256])
    c_kr = P.din("c_kr", [L, PAST, 32])
    s_f = P.din("s_f", [L, 4, 64, 128])
    s_b = P.din("s_b", [L, 4, 64, 128])
    consts = P.din("consts", [128, 12, 128])
    rope64 = P.din("rope64", [64, 2, TS])
    rope32 = P.din("rope32", [32, 2, TS])
    sel3 = P.din("sel3", [4, 2 + 2 * 128])
    foldm = P.din("foldm", [128, 2, 4, 2])

    y = P.dout("y", [T, D])
    o_k = P.dout("o_k", [2, L, 256, 128])
    o_v = P.dout("o_v", [2, L, 256, 128])
    o_ckv = P.dout("o_ckv", [2, L, 256, 256])
    o_kr = P.dout("o_kr", [2, L, 256, 32])
    o_sf = P.dout("o_sf", [2, L, 4, 64, 128])
    o_sb = P.dout("o_sb", [2, L, 4, 64, 128])

    dbg = None
    xs = P.dscr("xs", [T, D]).ap()
    ag1_in = [P.dscr("ag1_in%d" % l, [288, TS], BF16) for l in range(L)]
    ag1_out = [P.dscr("ag1_out%d" % l, [4 * 288, TS], BF16) for l in range(L)]
    agc_in = [P.dscr("agc_in%d" % l, [256, TS], BF16) for l in range(L)]
    agc_out = [P.dscr("agc_out%d" % l, [4 * 256, TS], BF16) for l in range(L)]
    ag2_in = [P.dscr("ag2_in%d" % l, [256, 258]) for l in range(L)]
    ag2_out = [P.dscr("ag2_out%d" % l, [4 * 256, 258]) for l in range(L)]
    pkv = P.dscr("pkv", [L, 288, TP], BF16).ap()
    agm_in = P.dscr("agm_in", [2 * 3, 768])
    agm_out = P.dscr("agm_out", [4 * 2 * 3, 768])

    CONS = P.sb("CONS", [128, 12, 128])
    R64 = P.sb("R64", [128, 2, TS])
    R32 = P.sb("R32", [32, 2, TS])
    SEL3 = P.sb("SEL3", [4, 2 + 256])
    FOLD = P.sb("FOLD", [128, 2, 4, 2])
    SMALL = P.sb("SMALL", [128, 64])
    HT = P.sb("HT", [128, 8, T], BF16)
    MERGED = P.sb("MERGED", [128, 8, T], BF16)
    GB = P.sb("GB", [128, L, 2, D])
    ACBC = P.sb("ACBC", [128, L, 2, 8, 2])
    WB = [P.sb("WB%d" % i, [128, 4096], BF16) for i in range(3)]
    ARENA_BYTES = 96 * 1024
    ARENA = P.sb("ARENA", [128, ARENA_BYTES // 4])
    PS2 = [P.st.enter_context(nc.psum_tensor("ps%d" % i, [128, 1024], F32)) for i in range(4)]
    PS = [PS2[i // 2][:, (i % 2) * 512:(i % 2 + 1) * 512] for i in range(8)]
    sems = [P.st.enter_context(nc.semaphore("s%d" % i)) for i in range(em.sems_needed())]
    em.bind_sems(sems)
    cc_sems = [P.st.enter_context(nc.semaphore("cc%d" % i)) for i in range(3 * L + 1)]
    block = P.st.enter_context(nc.Block())

    def ps(i, p=128, lo=0, n=512):
        return V(PS[i][0:p, lo:lo + n], ("PS", i))

    IDENT = V(CONS[:, 0, :], "CONS")
    ONES = lambda p=128, n=128: V(CONS[0:p, 7, 0:n], "CONS")
    PERM64 = V(CONS[0:64, 8, 0:64], "CONS")
    PERM32 = V(CONS[0:32, 9, 0:32], "CONS")
    BDONES = V(CONS[:, 10, :], "CONS")
    BDPERM = V(CONS[:, 11, :], "CONS")

    class Arena:
        def __init__(self):
            self.off = 0

        def reset(self):
            self.off = 0

        def alloc(self, name, p, shape, dt=F32):
            n = int(np.prod(shape))
            words = n if dt == F32 else (n + 1) // 2
            a = ARENA[0:p, self.off:self.off + words]
            self.off += words
            assert self.off * 4 <= ARENA_BYTES, (name, self.off * 4)
            if dt != F32:
                a = a.bitcast(dt)
                a = a[:, 0:n]
            if len(shape) > 1:
                names = " ".join("d%d" % i for i in range(len(shape)))
                a = a.rearrange("p (%s) -> p %s" % (names, names), **{"d%d" % i: shape[i] for i in range(1, len(shape))})
            return a

    AR = Arena()

    def new_phase(keep=0):
        em.barrier()
        AR.off = keep

    SC0 = AR.alloc("SC", 128, [8, 3])
    P.dma(V(SC0, "SC"), V(condT[:, :, :], "d.c"))
    P.dma(V(SEL3[:], "SEL3"), V(sel3[:, :], "d.sel3"))
    P.dma(V(CONS[:], "CONS"), V(consts[:, :, :], "d.consts"))
    P.dma(V(R64[0:64], "R64"), V(rope64[:, :, :], "d.r64"))
    P.dma(V(R64[64:128], "R64"), V(rope64[:, :, :], "d.r64"))
    P.dma(V(R32[:], "R32"), V(rope32[:, :, :], "d.r32"))
    P.dma(V(FOLD[:], "FOLD"), V(foldm[:, :, :, :], "d.fold"))
    P.dma(V(SMALL[0:64, 0:4], "SMALL"), V(gqk_c.rearrange("p l g -> p (l g)"), "d.s"))
    P.dma(V(SMALL[64:128, 0:4], "SMALL"), V(gqk_c.rearrange("p l g -> p (l g)"), "d.s"))
    P.dma(V(SMALL[:, 4:6], "SMALL"), V(g_gla_c[:, :], "d.s"))
    P.dma(V(SMALL[:, 8:12], "SMALL"), V(g_mq_c.rearrange("p l g -> p (l g)"), "d.s"))
    P.dma(V(SMALL[:, 12:16], "SMALL"), V(g_mkv_c.rearrange("p l g -> p (l g)"), "d.s"))
    P.dma(V(SMALL[:, 16:32], "SMALL"), V(g_pre_c.rearrange("p l c -> p (l c)"), "d.s"))

    def small(col, p=128):
        return V(SMALL[0:p, col:col + 1], "SMALL")

    wstate = {"i": 0}

    def load_w(src, prows, nchunk, ncols):
        k = wstate["i"] % 3
        wstate["i"] += 1
        assert nchunk * ncols <= 4096
        dst = WB[k][0:prows, 0:nchunk * ncols].rearrange("p (c n) -> p c n", n=ncols)
        P.em.dma("pool", dst, src.rearrange("(c p) n -> p c n", p=prows), reads=[], writes=[("WB", k)])
        return (k, ncols)

    def wv(kh, prows, c, lo, n):
        k, ncols = kh
        return V(WB[k][0:prows, c * ncols + lo:c * ncols + lo + n], ("WB", k))

    def htv(c, t0, n):
        return V(HT[:, c, t0:t0 + n], [("HT", t) for t in range(t0 // 128, (t0 + n + 127) // 128)])

    def run_jobs(jobs):
        handles = [None] * len(jobs)
        for j in range(min(2, len(jobs))):
            handles[j] = jobs[j][0]()
        for i, (_, comp) in enumerate(jobs):
            if i + 2 < len(jobs):
                handles[i + 2] = jobs[i + 2][0]()
            comp(handles[i])

    def phase_mod():
        SC = SC0
        SCB = AR.alloc("SCB", 128, [8, 3], BF16)
        MR = AR.alloc("MR", 4, [3 * D])
        WM = [AR.alloc("WM%d" % i, 128, [8, 768], BF16) for i in range(2)]
        STG = AR.alloc("STG", 3, [2, 768])
        GPB = AR.alloc("GPB", 128, [D])
        P.act(V(SCB, "SCB"), V(SC, "SC"), AF.Silu)
        for l in range(L):
            P.dma(V(WM[l], ("WM", l)), V(w_mod[l, :, :].rearrange("(c p) n -> p c n", p=128), "d.wm"), eng="pool")
            for hf in range(2):
                b = P.psum("x")
                P.mm(ps(b, 3, 0, 384), [(V(SCB[:, c, :], "SCB"), V(WM[l][:, c, hf * 384:(hf + 1) * 384], ("WM", l))) for c in range(8)])
                P.copy(V(STG[:, l, hf * 384:(hf + 1) * 384], "STG"), ps(b, 3, 0, 384))
        P.dma(V(agm_in.ap().rearrange("(l g) c -> g l c", l=2), "AGM"), V(STG, "STG"))
        em.collective(cc_sems[3 * L],
                      lambda e: e.collective_compute("AllGather", ALU.bypass, replica_groups=[[0, 1, 2, 3], [4, 5, 6, 7]],
                                                     ins=[agm_in.ap().opt()], outs=[agm_out.ap().opt()]),
                      reads=["AGM"], writes=["AGMO"])
        gath = agm_out.ap().rearrange("(r l g) c -> l g r c", r=4, l=2, g=3)
        for l in range(L):
            P.dma(V(MR[0:3, :].rearrange("g (r c) -> g r c", r=4), "MR"), V(gath[l], "AGMO"))
            P.dma(V(MR[3:4, :], "MR"), V(b_mod[l:l + 1, :], "d.bm"))
            b = P.psum("x")
            for j in range(16):
                P.mm(ps(b, 128, 2 * j, 2), [(V(MR[0:4, j * 128:(j + 1) * 128], "MR"), V(SEL3[0:4, 0:2], "SEL3"))])
            pv = PS[b][:, 0:32].rearrange("p (j g) -> p j g", g=2)
            P.copy(V(ACBC[:, l, 1, :, :], "ACBC"), V(pv[:, 0:8, :], ("PS", b)))
            for g in range(2):
                P.stt(V(ACBC[:, l, 0, :, g], "ACBC"), V(pv[:, 8:16, g], ("PS", b)), 1.0, V(SMALL[:, 16 + 8 * l:24 + 8 * l], "SMALL"), ALU.add, ALU.mult)
            P.dma(V(GPB, "GPB"), V(g_post[l:l + 1, :].partition_broadcast(128), "d.gp"))
            for g in range(2):
                for hf in range(2):
                    b2 = P.psum("x")
                    P.mm(ps(b2), [(V(SEL3[0:4, 2 + 128 * g:2 + 128 * (g + 1)], "SEL3"), V(MR[0:4, 2 * D + hf * 512:2 * D + (hf + 1) * 512], "MR"))])
                    P.tt(V(GB[:, l, g, hf * 512:(hf + 1) * 512], "GB"), ps(b2), V(GPB[:, hf * 512:(hf + 1) * 512], "GPB"), ALU.mult)

    def phase_ht(l):
        src = xin if l == 0 else xs
        XB = [AR.alloc("XB%d" % i, 128, [D]) for i in range(3)]
        JUNK = AR.alloc("JUNK", 128, [D], BF16)
        ST = AR.alloc("ST", 128, [NT, 2])
        P.memset(V(ST, [("ST", t) for t in range(NT)]), 0.0)
        def stage_a(t):
            xb = XB[t % 3]
            kx = ("XB", t % 3)
            P.dma(V(xb, kx), V(src[t * 128:(t + 1) * 128, :], "d.x"))
            P.act(V(JUNK, "JUNK"), V(xb, kx), AF.Square, accum=V(ST[:, t, 0:1], ("ST", t)))
            P.act(V(ST[:, t, 1:2], ("ST", t)), V(ST[:, t, 0:1], ("ST", t)), AF.Sqrt, bias=EPS, scale=1.0 / D)
            P.recip(V(ST[:, t, 1:2], ("ST", t)), V(ST[:, t, 1:2], ("ST", t)))
            P.ts(V(xb, kx), V(xb, kx), V(ST[:, t, 1:2], ("ST", t)), ALU.mult)

        def stage_b(t):
            g = 0 if t < 4 else 1
            xb = XB[t % 3]
            kx = ("XB", t % 3)
            for hf in range(2):
                b = P.psum("x")
                for c4 in range(4):
                    c = hf * 4 + c4
                    P.transpose(ps(b, 128, c4 * 128, 128), V(xb[:, c * 128:(c + 1) * 128], kx), IDENT)
                for c4 in range(4):
                    c = hf * 4 + c4
                    dst = V(HT[:, c, t * 128:(t + 1) * 128], ("HT", t))
                    if hf == 0:
                        P.act(dst, ps(b, 128, c4 * 128, 128), AF.Identity, bias=V(ACBC[:, l, 1, c, g:g + 1], "ACBC"), scale=V(ACBC[:, l, 0, c, g:g + 1], "ACBC"))
                    else:
                        P.ts(dst, ps(b, 128, c4 * 128, 128), V(ACBC[:, l, 0, c, g:g + 1], "ACBC"), ALU.mult, V(ACBC[:, l, 1, c, g:g + 1], "ACBC"), ALU.add)
        stage_a(0)
        stage_a(1)
        for t in range(NT):
            stage_b(t)
            if t + 2 < NT:
                stage_a(t + 2)

    def proj2(k, prows_unused, col_lo, m, blk, bank=None):
        t0, n = BLOCKS[blk]
        b = P.psum("m") if bank is None else bank
        P.mm(ps(b, m, 0, n), [(wv(k, 128, c, col_lo, m), htv(c, t0, n)) for c in range(8)])
        return b

    def head_norm_rope(b, m, blk, gcol, out_bf, out_f32, tmp, rope, key_out, scale_extra=1.0):
        SQ, RS, QN, T1 = tmp
        onesm = V(CONS[0:m, 7, 0:m], "CONS")
        P.act(V(SQ[0:m, :], "SQ"), ps(b, m), AF.Square)
        b2 = P.psum("x")
        P.mm(ps(b2, m), [(onesm, V(SQ[0:m, :], "SQ"))])
        P.act(V(RS[0:m, :], "RS"), ps(b2, m), AF.Sqrt, bias=EPS, scale=1.0 / m)
        P.recip(V(RS[0:m, :], "RS"), V(RS[0:m, :], "RS"))
        if rope is None:
            if out_f32 is not None:
                P.stt(out_f32, ps(b, m), gcol, V(RS[0:m, :], "RS"), ALU.mult, ALU.mult)
                P.copy(out_bf, out_f32, eng="pool")
            else:
                P.stt(out_bf, ps(b, m), gcol, V(RS[0:m, :], "RS"), ALU.mult, ALU.mult)
            return
        perm, RT, s0 = rope
        P.stt(V(QN[0:m, :], "QN"), ps(b, m), gcol, V(RS[0:m, :], "RS"), ALU.mult, ALU.mult)
        b3 = P.psum("x")
        P.mm(ps(b3, m), [(perm, V(QN[0:m, :], "QN"))])
        P.tt(V(T1[0:m, :], "T1"), ps(b3, m), V(RT[0:m, 1, s0:s0 + 512], "ROPE"), ALU.mult)
        P.tt(V(QN[0:m, :], "QN"), V(QN[0:m, :], "QN"), V(RT[0:m, 0, s0:s0 + 512], "ROPE"), ALU.mult, eng="pool")
        P.tt(out_bf, V(QN[0:m, :], "QN"), V(T1[0:m, :], "T1"), ALU.add)

    def pair_norm_rope(b, gcol, out_bf, out_f32, tmp, s0):
        SQ, RS, QN, T1 = tmp
        P.act(V(SQ, "SQ"), ps(b), AF.Square)
        b2 = P.psum("x")
        P.mm(ps(b2), [(BDONES, V(SQ, "SQ"))])
        P.act(V(RS, "RS"), ps(b2), AF.Sqrt, bias=EPS, scale=1.0 / 64)
        P.recip(V(RS, "RS"), V(RS, "RS"))
        if s0 is None:
            if out_f32 is not None:
                P.stt(out_f32, ps(b), gcol, V(RS, "RS"), ALU.mult, ALU.mult)
                P.copy(out_bf, out_f32, eng="pool")
            else:
                P.stt(out_bf, ps(b), gcol, V(RS, "RS"), ALU.mult, ALU.mult)
            return
        P.stt(V(QN, "QN"), ps(b), gcol, V(RS, "RS"), ALU.mult, ALU.mult)
        b3 = P.psum("x")
        P.mm(ps(b3), [(BDPERM, V(QN, "QN"))])
        P.tt(V(T1, "T1"), ps(b3), V(R64[:, 1, s0:s0 + 512], "ROPE"), ALU.mult)
        P.tt(V(QN, "QN"), V(QN, "QN"), V(R64[:, 0, s0:s0 + 512], "ROPE"), ALU.mult, eng="pool")
        P.tt(out_bf, V(QN, "QN"), V(T1, "T1"), ALU.add)

    def staggered(gens):
        live = []
        it = iter(gens)
        while True:
            g_ = next(it, None)
            if g_ is not None:
                live.append(g_)
            elif not live:
                break
            nxt = []
            for g2 in live:
                try:
                    next(g2)
                    nxt.append(g2)
                except StopIteration:
                    pass
            live = nxt

    def pair_chain(proj_fn, gcol, out_bf, out_f32, tmp, s0, ti):
        SQ, RS, QN, T1 = tmp
        kS, kR, kQ, kT = ("SQ", ti), ("RS", ti), ("QN", ti), ("T1", ti)
        b = proj_fn()
        yield
        P.act(V(SQ, kS), ps(b), AF.Square)
        b2 = P.psum("x")
        P.mm(ps(b2), [(BDONES, V(SQ, kS))])
        yield
        P.act(V(RS, kR), ps(b2), AF.Sqrt, bias=EPS, scale=1.0 / 64)
        P.recip(V(RS, kR), V(RS, kR))
        if s0 is None:
            if out_f32 is not None:
                P.stt(out_f32, ps(b), gcol, V(RS, kR), ALU.mult, ALU.mult)
                P.copy(out_bf, out_f32, eng="pool")
            else:
                P.stt(out_bf, ps(b), gcol, V(RS, kR), ALU.mult, ALU.mult)
            return
        P.stt(V(QN, kQ), ps(b), gcol, V(RS, kR), ALU.mult, ALU.mult)
        b3 = P.psum("x")
        P.mm(ps(b3), [(BDPERM, V(QN, kQ))])
        yield
        P.tt(V(T1, kT), ps(b3), V(R64[:, 1, s0:s0 + 512], "ROPE"), ALU.mult)
        P.tt(V(QN, kQ), V(QN, kQ), V(R64[:, 0, s0:s0 + 512], "ROPE"), ALU.mult, eng="pool")
        P.tt(out_bf, V(QN, kQ), V(T1, kT), ALU.add)

    def norm_tmps():
        return [AR.alloc(nm, 128, [512]) for nm in ("SQ", "RS", "QN", "T1")]

    def phase_kv(l, QT, YA):
        tmp = norm_tmps()
        tmp_b = [AR.alloc(nm + "b", 128, [512]) for nm in ("SQ", "RS", "QN", "T1")]
        KSTG = AR.alloc("KSTG", 128, [512])
        KTS = AR.alloc("KTS", 128, [TS], BF16)
        TOK = [AR.alloc("TOK%d" % i, 128, [256]) for i in range(2)]
        VB = AR.alloc("VB", 128, [8, 128], BF16)
        CS = [AR.alloc("CS%d" % i, 128, [512]) for i in range(2)]
        CB = AR.alloc("CB", 128, [2, TS], BF16)
        CPB = AR.alloc("CPB", 128, [2, TP], BF16)
        KRS = AR.alloc("KRS", 32, [512])
        KRB = AR.alloc("KRB", 32, [T], BF16)
        agin = ag1_in[l].ap()
        kag = ("AG1", l)

        def job_ka_load():
            return load_w(w_in[l, :, O_KA:O_KA + 128], 128, 8, 128)

        def job_ka(k):
            for blk in range(3):
                b = proj2(k, 128, 0, 128, blk)
                gcol = small(2 * l + 1)
                if blk == 0:
                    pair_norm_rope(b, gcol, V(KTP, "KTP"), V(KSTG, "KSTG"), tmp, None)
                else:
                    s0 = (blk - 1) * 512
                    pair_norm_rope(b, gcol, V(KTS[:, s0:s0 + 512], "KTS"), None, tmp, s0)
            for t in range(4):
                bt = P.psum("x")
                P.transpose(ps(bt, 128, 0, 128), V(KSTG[:, t * 128:(t + 1) * 128], "KSTG"), IDENT)
                tk = TOK[t % 2]
                P.copy(V(tk[:, 0:128], ("TOK", t % 2)), ps(bt, 128, 0, 128))
                P.dma(V(o_k[t // 2, l, (t % 2) * 128:(t % 2 + 1) * 128, :], ("o.k", l, t)), V(tk[:, 0:128], ("TOK", t % 2)))
            P.dma(V(agin[0:128, :], kag), V(KTS, "KTS"))

        def job_va_load():
            return load_w(w_in[l, :, O_VA:O_VA + 128], 128, 8, 128)

        def job_va(k):
            for t in range(NT):
                b = P.psum("m")
                P.mm(ps(b, 128, 0, 128), [(htv(c, t * 128, 128), wv(k, 128, c, 0, 128)) for c in range(8)])
                if t < 4:
                    tk = TOK[t % 2]
                    P.copy(V(tk[:, 0:128], ("TOK", t % 2)), ps(b, 128, 0, 128))
                    P.dma(V(o_v[t // 2, l, (t % 2) * 128:(t % 2 + 1) * 128, :], ("o.v", l, t)), V(tk[:, 0:128], ("TOK", t % 2)))
                    P.copy(V(VAP[:, t, :, 0:64], "VAP"), V(PS[b][:, 0:128].rearrange("p (g d) -> p g d", g=2), ("PS", b)), eng="act")
                else:
                    P.copy(V(VB[:, t - 4, :], "VB"), ps(b, 128, 0, 128), eng="act")
            P.dma(V(agin[128:256, :].rearrange("r (a c) -> (r a) c", c=128).rearrange("(t p) c -> p t c", p=128), kag), V(VB, "VB"))

        def job_kv_load():
            return load_w(w_in[l, :, O_KV:O_KV + 288], 128, 8, 288)

        def job_kv(k):
            for blk in range(3):
                t0, n = BLOCKS[blk]
                bs = [proj2(k, 128, 128 * c2, 128, blk) for c2 in range(2)]
                SQ, RS, QN, T1 = tmp
                for c2 in range(2):
                    P.act(V((SQ if c2 == 0 else T1), ("SQ" if c2 == 0 else "T1")), ps(bs[c2]), AF.Square)
                b2 = P.psum("x")
                P.mm(ps(b2), [(ONES(), V(SQ, "SQ")), (ONES(), V(T1, "T1"))])
                P.act(V(RS, "RS"), ps(b2), AF.Sqrt, bias=EPS, scale=1.0 / 256)
                P.recip(V(RS, "RS"), V(RS, "RS"))
                for c2 in range(2):
                    gcol = small(12 + 2 * l + c2)
                    if blk == 0:
                        P.stt(V(CS[c2], ("CS", c2)), ps(bs[c2]), gcol, V(RS, "RS"), ALU.mult, ALU.mult)
                        P.copy(V(CPB[:, c2, :], "CPB"), V(CS[c2], ("CS", c2)), eng="pool")
                    else:
                        s0 = (blk - 1) * 512
                        P.stt(V(CB[:, c2, s0:s0 + 512], "CB"), ps(bs[c2]), gcol, V(RS, "RS"), ALU.mult, ALU.mult)
                if blk == 0:
                    for t in range(4):
                        bt = P.psum("x")
                        for c2 in range(2):
                            P.transpose(ps(bt, 128, 128 * c2, 128), V(CS[c2][:, t * 128:(t + 1) * 128], ("CS", c2)), IDENT)
                        tk = TOK[t % 2]
                        P.copy(V(tk, ("TOK", t % 2)), ps(bt, 128, 0, 256))
                        P.dma(V(o_ckv[t // 2, l, (t % 2) * 128:(t % 2 + 1) * 128, :], ("o.c", l, t)), V(tk, ("TOK", t % 2)))
                    P.dma(V(pkv[l, 0:256, :].rearrange("(c p) t -> p c t", p=128), ("PKV", l)), V(CPB, "CPB"))
                b = proj2(k, 128, 256, 32, blk)
                if blk == 0:
                    P.copy(V(KRS, "KRS"), ps(b, 32), eng="act")
                    P.copy(V(KRB[:, 0:512], "KRB"), V(KRS, "KRS"), eng="pool")
                    for t in range(4):
                        bt = P.psum("x")
                        P.transpose(ps(bt, 128, 0, 32), V(KRS[:, t * 128:(t + 1) * 128], "KRS"), V(CONS[0:32, 0, 0:32], "CONS"))
                        tk = TOK[t % 2]
                        P.copy(V(tk[:, 0:32], ("TOK", t % 2)), ps(bt, 128, 0, 32))
                        P.dma(V(o_kr[t // 2, l, (t % 2) * 128:(t % 2 + 1) * 128, :], ("o.r", l, t)), V(tk[:, 0:32], ("TOK", t % 2)))
                    P.dma(V(pkv[l, 256:288, :], ("PKV", l)), V(KRB[:, 0:512], "KRB"))
                else:
                    s0 = (blk - 1) * 512
                    SQ, RS, QN, T1 = tmp
                    P.copy(V(QN[0:32, :], "QN"), ps(b, 32), eng="act")
                    b3 = P.psum("x")
                    P.mm(ps(b3, 32), [(PERM32, V(QN[0:32, :], "QN"))])
                    P.tt(V(T1[0:32, :], "T1"), ps(b3, 32), V(R32[:, 1, s0:s0 + 512], "ROPE"), ALU.mult)
                    P.tt(V(QN[0:32, :], "QN"), V(QN[0:32, :], "QN"), V(R32[:, 0, s0:s0 + 512], "ROPE"), ALU.mult, eng="pool")
                    P.tt(V(KRB[:, 512 + s0:512 + s0 + 512], "KRB"), V(QN[0:32, :], "QN"), V(T1[0:32, :], "T1"), ALU.add)
            P.dma(V(agc_in[l].ap()[:, :].rearrange("(c p) t -> p c t", p=128), ("AGC", l)), V(CB, "CB"))
            P.dma(V(agin[256:288, :], kag), V(KRB[:, 512:T], "KRB"))

        extra = gqa_proj_jobs(l, QT, YA, [tmp, tmp_b])
        h_ka = job_ka_load()
        h_va = job_va_load()
        job_ka(h_ka)
        h_kv = job_kv_load()
        job_va(h_va)
        hx = [ld() for ld, _ in extra]
        job_kv(h_kv)
        if STAGE >= 1:
            grp = [[0, 1, 2, 3], [4, 5, 6, 7]]
            em.collective(cc_sems[3 * l],
                          lambda e: e.collective_compute("AllGather", ALU.bypass, replica_groups=grp,
                                                         ins=[ag1_in[l].ap().opt()], outs=[ag1_out[l].ap().opt()]),
                          reads=[kag], writes=[("AG1O", l)])
            em.collective(cc_sems[3 * l + 1],
                          lambda e: e.collective_compute("AllGather", ALU.bypass, replica_groups=grp,
                                                         ins=[agc_in[l].ap().opt()], outs=[agc_out[l].ap().opt()]),
                          reads=[("AGC", l)], writes=[("AGCO", l)])
        for (_, comp), h in zip(extra, hx):
            comp(h)


    mstate = {'first': True}

    def merge_branch(l, bi, Yv, nch, prow):
        SG4 = AR.alloc("SG4", 128, [4, T], BF16)
        TB = [AR.alloc("TB%d" % i, 128, [512]) for i in range(2)]
        o_m = (O_M1, O_M2, O_M3)[bi]
        jobs = []
        for half in range(2):
            def load_m(half=half):
                return load_w(w_in[l, :, o_m + half * 512:o_m + (half + 1) * 512], 128, 8, 512)

            def comp_m(k, half=half):
                for c4 in range(4):
                    for blk in range(3):
                        t0, n = BLOCKS[blk]
                        b = proj2(k, 128, 128 * c4, 128, blk)
                        P.act(V(SG4[:, c4, t0:t0 + n], ("SG4", c4, blk)), ps(b), AF.Sigmoid)

            def load_o(half=half):
                return load_w(w_o[bi][l, :, half * 512:(half + 1) * 512], prow, nch, 512)

            def comp_o(k, half=half):
                for c4 in range(4):
                    oc = half * 4 + c4
                    for blk in range(3):
                        t0, n = BLOCKS[blk]
                        b = P.psum("m")
                        P.mm(ps(b), [(wv(k, prow, ch, c4 * 128, 128), Yv(ch, blk)) for ch in range(nch)])
                        mg = V(MERGED[:, oc, t0:t0 + n], ("MG", oc, blk))
                        sg = V(SG4[:, c4, t0:t0 + n], ("SG4", c4, blk))
                        if mstate['first']:
                            P.tt(mg, ps(b), sg, ALU.mult)
                        else:
                            tb = V(TB[(c4 + blk) % 2], ("TB", (c4 + blk) % 2))
                            P.tt(tb, ps(b), sg, ALU.mult)
                            P.tt(mg, mg, tb, ALU.add, eng="pool")
            jobs.append((load_m, comp_m))
            jobs.append((load_o, comp_o))
        run_jobs(jobs)
        mstate['first'] = False


    def attention_stream2(items, pt2, depth=2):
        dbl = [0, 1, 3]
        n = len(items)
        cur = {}
        for j in range(min(depth, n)):
            cur[j] = dbl[j % 3]
            items[j][0](cur[j])
        for i in range(n):
            if i + depth < n:
                cur[i + depth] = dbl[(i + depth) % 3]
                items[i + depth][0](cur[i + depth])
            d_ = cur.pop(i)
            ring = i % len(pt2)
            ptv = V(pt2[ring], ("PT", ring))
            P.act(ptv, V(PS2[d_][:, :], [("PS", 2 * d_), ("PS", 2 * d_ + 1)]), AF.Exp, scale=items[i][1])
            items[i][2](ring)
            if items[i][3] is not None:
                items[i][3]()

    def attention_stream(items, depth=2, nring=4):
        n = len(items)
        banks = {}
        for j in range(min(depth, n)):
            banks[j] = P.psum("m")
            items[j][0](banks[j])
        for i in range(n):
            if i + depth < n:
                banks[i + depth] = P.psum("m")
                items[i + depth][0](banks[i + depth])
            items[i][1](banks.pop(i), i % nring)
            items[i][2](i % nring)
            if items[i][3] is not None:
                items[i][3]()

    def gqa_proj_jobs(l, QT, YA, tmps):
        def mk(is_q):
            o = O_QA if is_q else O_GA

            def load():
                return load_w(w_in[l, :, o:o + 512], 128, 8, 512)

            def comp(k):
                def gate_chain(p, blk):
                    t0, n = BLOCKS[blk]
                    b = proj2(k, 128, 128 * p, 128, blk)
                    yield
                    P.act(V(YA[:, p, t0:t0 + n], ("YA", p, blk)), ps(b), AF.Silu)
                gens = []
                i = 0
                for p in range(4):
                    for blk in range(3):
                        t0, n = BLOCKS[blk]
                        if is_q:
                            gens.append(pair_chain(lambda p=p, blk=blk: proj2(k, 128, 128 * p, 128, blk), small(2 * l),
                                                   V(QT[:, p, t0:t0 + n], ("QT", p, blk)), None, tmps[i % 2],
                                                   None if blk == 0 else (blk - 1) * 512, i % 2))
                        else:
                            gens.append(gate_chain(p, blk))
                        i += 1
                staggered(gens)
            return load, comp
        return [mk(True), mk(False)]

    def phase_gqa(l, QT, YA):
        off_a = AR.off
        KA = AR.alloc("KA", 128, [NKS], BF16)
        KB = AR.alloc("KB", 128, [NKS], BF16)
        VAS = AR.alloc("VAS", 128, [36, 128], BF16)
        KAP = AR.alloc("KAP", 128, [2, TP], BF16)
        KBP = AR.alloc("KBP", 128, [2, TP], BF16)
        PT2 = [AR.alloc("PT%d" % i, 128, [1024], BF16) for i in range(3)]
        PT = [PT2[i][:, 0:512] for i in range(3)]
        RC = AR.alloc("RC", 64, [512])
        TM = AR.alloc("TM", 64, [512])
        TM2 = AR.alloc("TM2", 128, [512])
        CK = [AR.alloc("CK%d" % i, 128, [64]) for i in range(2)]
        P.memset(V(VAS, "VAS"), 1.0)
        P.memset(V(KA, "KA"), 0.0)
        P.memset(V(KB, "KB"), 0.0)
        P.memset(V(KAP, "KAP"), 0.0)
        P.memset(V(KBP, "KBP"), 0.0)
        for g in range(2):
            src = V(KTP[64 * g:64 * g + 64, :], "KTP")
            P.copy(V(KAP[0:64, g, :], "KAP"), src, eng="act")
            P.copy(V(KBP[64:128, g, :], "KBP"), src, eng="act")

        def finish_head(ob, h, blk, t0, n):
            p, hh = divmod(h, 2)
            P.recip(V(RC[:, 0:n], "RC"), V(PS[ob][64:128, 0:n], ("PS", ob)))
            P.tt(V(TM[:, 0:n], "TM"), V(PS[ob][0:64, 0:n], ("PS", ob)), V(RC[:, 0:n], "RC"), ALU.mult)
            if hh == 0:
                ya = V(YA[0:64, p, t0:t0 + n], ("YA", p, blk))
                P.tt(ya, V(TM[:, 0:n], "TM"), ya, ALU.mult, eng="pool")
            else:
                P.copy(V(TM2[64:128, 0:n], "TM2"), V(TM[:, 0:n], "TM"), eng="pool")
                ya = V(YA[64:128, p, t0:t0 + n], ("YA", p, blk))
                P.tt(ya, V(TM2[64:128, 0:n], "TM2"), ya, ALU.mult, eng="pool")

        ago = ag1_out[l].ap()
        for g in range(2):
            for t in range(4):
                ck = V(CK[t % 2], ("CK", t % 2))
                P.dma(ck, V(c_k[l, t * 128:(t + 1) * 128, g * 64:(g + 1) * 64], "d.ck"))
                bt = P.psum("x")
                P.transpose(ps(bt, 64, 0, 128), ck, IDENT)
                P.copy(V(KA[0:64, t * 128:(t + 1) * 128], "KA"), ps(bt, 64, 0, 128))
                P.copy(V(KB[64:128, t * 128:(t + 1) * 128], "KB"), ps(bt, 64, 0, 128), eng="act")
            for r in range(4):
                srck = V(ago[r * 288 + g * 64:r * 288 + (g + 1) * 64, :], ("AG1O", l))
                P.dma(V(KA[0:64, PAST + r * TS:PAST + (r + 1) * TS], "KA"), srck)
                P.dma(V(KB[64:128, PAST + r * TS:PAST + (r + 1) * TS], "KB"), srck)
            P.dma(V(VAS[:, 0:4, 0:64], "VAS"), V(c_v[l, :, g * 64:(g + 1) * 64].rearrange("(t p) d -> p t d", p=128), "d.cv"), eng="pool")
            for r in range(4):
                src = ago[r * 288 + 128:r * 288 + 256, :].rearrange("r (a c) -> (r a) c", c=128)[:, g * 64:(g + 1) * 64].rearrange("(t p) d -> p t d", p=128)
                P.dma(V(VAS[:, 4 + 8 * r:4 + 8 * (r + 1), 0:64], "VAS"), V(src, ("AG1O", l)))
            items = []

            def add_head(h, blk, t0, n, kview, vview, ntile):
                ob = P.psum("a")
                qv = V(QT[:, h // 2, t0:t0 + n], ("QT", h // 2, blk))
                for kt in range(ntile):
                    def S(bank, kt=kt):
                        P.mm(ps(bank, 128, 0, n), [(kview(kt), qv)])

                    def E(bank, ring):
                        P.act(V(PT[ring][:, 0:n], ("PT", ring)), ps(bank, 128, 0, n), AF.Exp, scale=0.125)

                    def PV(ring, kt=kt):
                        P.mm1(ps(ob, 128, 0, n), vview(kt), V(PT[ring][:, 0:n], ("PT", ring)), start=(kt == 0), stop=(kt == ntile - 1),
                              inc=(kt == ntile - 1))
                    fin = (lambda: finish_head(ob, h, blk, t0, n)) if kt == ntile - 1 else None
                    items.append((S, E, PV, fin))
            for s_ in range(2):
                for hh in range(4):
                    h = g * 4 + hh
                    kp = KAP if h % 2 == 0 else KBP
                    add_head(h, 0, s_ * 256, 256,
                             lambda kt, s_=s_, kp=kp: V(kp[:, g, s_ * 256 + kt * 128:s_ * 256 + (kt + 1) * 128], "KAP"),
                             lambda kt, s_=s_: V(VAP[:, s_ * 2 + kt, g, :], "VAP"), 2)
            attention_stream(items, nring=3)
            items2 = []
            for qb in range(2):
                for hh in range(4):
                    h = g * 4 + hh
                    kk = KA if h % 2 == 0 else KB
                    ob = P.psum("a")
                    t0 = TP + qb * 512
                    qv = V(QT[:, h // 2, t0:t0 + 512], ("QT", h // 2, 1 + qb))
                    for pi in range(18):
                        def S(d_, pi=pi, kk=kk, qv=qv):
                            for j in range(2):
                                kt = 2 * pi + j
                                P.mm(ps(2 * d_ + j), [(V(kk[:, kt * 128:(kt + 1) * 128], "KA"), qv)])

                        def PV(ring, pi=pi, ob=ob):
                            for j in range(2):
                                kt = 2 * pi + j
                                P.mm1(ps(ob), V(VAS[:, kt, :], "VAS"), V(PT2[ring][:, j * 512:(j + 1) * 512], ("PT", ring)),
                                      start=(kt == 0), stop=(kt == 35), inc=(kt == 35))
                        fin = (lambda ob=ob, h=h, qb=qb, t0=t0: finish_head(ob, h, 1 + qb, t0, 512)) if pi == 17 else None
                        items2.append((S, 0.125, PV, fin))
            attention_stream2(items2, PT2)
        em.barrier()
        AR.off = off_a
        merge_branch(l, 0, lambda ch, blk: V(YA[:, ch, BLOCKS[blk][0]:BLOCKS[blk][0] + BLOCKS[blk][1]], ("YA", ch, blk)), 4, 128)


    NKA = TP + NKS

    def rope_from(QNv_ap, m, perm, RT, s0, out_bf, tmp):
        SQ, RS, QN, T1 = tmp
        b3 = P.psum("x")
        P.mm(ps(b3, m), [(perm, V(QNv_ap, "QN"))])
        P.tt(V(T1[0:m, :], "T1"), ps(b3, m), V(RT[0:m, 1, s0:s0 + 512], "ROPE"), ALU.mult)
        P.tt(V(QNv_ap, "QN"), V(QNv_ap, "QN"), V(RT[0:m, 0, s0:s0 + 512], "ROPE"), ALU.mult, eng="pool")
        P.tt(out_bf, V(QNv_ap, "QN"), V(T1[0:m, :], "T1"), ALU.add)

    def phase_mla(l):
        YC = AR.alloc("YC", 128, [4, T], BF16)
        QC = AR.alloc("QC", 96, [4, T], BF16)
        KN = AR.alloc("KN", 96, [NKA], BF16)
        KR = KN[64:96, :]
        CKC = AR.alloc("CKC", 128, [2, 512], BF16)
        ONESB = AR.alloc("ONESB", 128, [128], BF16)
        P.memset(V(ONESB, "ONESB"), 1.0)
        off_a = AR.off
        QNT = AR.alloc("QNT", 128, [2, T], BF16)
        CKS = [AR.alloc("CKS%d" % i, 128, [384]) for i in range(2)]
        for i in range(2):
            P.memset(V(CKS[i], ("CKS", i)), 0.0)
        tmp = norm_tmps()
        SQ, RS, QN, T1 = tmp
        ago = ag1_out[l].ap()
        for t in range(4):
            cs = CKS[t % 2]
            kc = ("CKS", t % 2)
            P.dma(V(cs[:, 0:256], kc), V(c_ckv[l, t * 128:(t + 1) * 128, :], "d.cc"))
            P.dma(V(cs[:, 256:288], kc), V(c_kr[l, t * 128:(t + 1) * 128, :], "d.cr"))
            bt = P.psum("x")
            for c2 in range(2):
                P.transpose(ps(bt, 128, c2 * 128, 128), V(cs[:, c2 * 128:(c2 + 1) * 128], kc), IDENT)
            P.transpose(ps(bt, 128, 256, 128), V(cs[:, 256:384], kc), IDENT)
            P.copy(V(CKC[:, :, t * 128:(t + 1) * 128], "CKC"), V(PS[bt][:, 0:256].rearrange("p (c t) -> p c t", c=2), ("PS", bt)))
            P.copy(V(KR[:, TP + t * 128:TP + (t + 1) * 128], "KN"), ps(bt, 32, 256, 128), eng="act")
        P.dma(V(KR[:, 0:TP], "KN"), V(pkv[l, 256:288, :], ("PKV", l)))
        for r in range(4):
            P.dma(V(KR[:, TP + PAST + r * TS:TP + PAST + (r + 1) * TS], "KN"), V(ago[r * 288 + 256:r * 288 + 288, :], ("AG1O", l)))

        if MLA_CUT <= 1:
            return
        def load_ql():
            return load_w(w_in[l, :, O_QL:O_QL + 256], 128, 8, 256)

        def comp_ql(k):
            for blk in range(3):
                t0, n = BLOCKS[blk]
                bs = [proj2(k, 128, 128 * c2, 128, blk) for c2 in range(2)]
                for c2 in range(2):
                    P.act(V((SQ if c2 == 0 else T1), ("SQ" if c2 == 0 else "T1")), ps(bs[c2]), AF.Square)
                b2 = P.psum("x")
                P.mm(ps(b2), [(ONES(), V(SQ, "SQ")), (ONES(), V(T1, "T1"))])
                P.act(V(RS, "RS"), ps(b2), AF.Sqrt, bias=EPS, scale=1.0 / 256)
                P.recip(V(RS, "RS"), V(RS, "RS"))
                for c2 in range(2):
                    P.stt(V(QNT[:, c2, t0:t0 + n], ("QNT", blk)), ps(bs[c2]), small(8 + 2 * l + c2), V(RS, "RS"), ALU.mult, ALU.mult)

        def load_uq():
            return load_w(w_uq[l, :, :], 128, 2, 384)

        def comp_uq(k):
            for h in range(4):
                for blk in range(3):
                    t0, n = BLOCKS[blk]
                    b = P.psum("m")
                    P.mm(ps(b, 64), [(wv(k, 128, c2, h * 96, 64), V(QNT[:, c2, t0:t0 + n], ("QNT", blk))) for c2 in range(2)])
                    P.copy(V(QC[0:64, h, t0:t0 + n], ("QC", h, blk)), ps(b, 64), eng="act")
                    b = P.psum("m")
                    P.mm(ps(b, 32), [(wv(k, 128, c2, h * 96 + 64, 32), V(QNT[:, c2, t0:t0 + n], ("QNT", blk))) for c2 in range(2)])
                    dst = V(QC[64:96, h, t0:t0 + n], ("QC", h, blk))
                    if blk == 0:
                        P.copy(dst, ps(b, 32), eng="act")
                    else:
                        P.copy(V(QN[0:32, :], "QN"), ps(b, 32), eng="act")
                        rope_from(QN[0:32, :], 32, PERM32, R32, (blk - 1) * 512, V(SQ[0:32, :], "SQ"), tmp)
                        P.copy(dst, V(SQ[0:32, :], "SQ"), eng="act")

        def mk_gc(i):
            def load():
                return load_w(w_in[l, :, O_GC + i * 256:O_GC + (i + 1) * 256], 128, 8, 256)

            def comp(k):
                for cc in range(2):
                    ch = 2 * i + cc
                    for blk in range(3):
                        t0, n = BLOCKS[blk]
                        b = proj2(k, 128, 128 * cc, 128, blk)
                        P.act(V(YC[:, ch, t0:t0 + n], ("YC", ch, blk)), ps(b), AF.Silu)
            return load, comp
        jl = [(load_ql, comp_ql), (load_uq, comp_uq), mk_gc(0), mk_gc(1)]
        run_jobs(jl[:MLA_JOBS])

        if MLA_CUT <= 2:
            return
        VC = AR.alloc("VC", 128, [NKA // 128, 128], BF16)
        CST = [AR.alloc("CST%d" % i, 128, [2, 512], BF16) for i in range(2)]
        PT = [AR.alloc("PT%d" % i, 128, [512], BF16) for i in range(4)]
        RC = AR.alloc("RC", 128, [512])
        TM = AR.alloc("TM", 128, [512])
        PACC = [AR.alloc("PACC%d" % i, 128, [512]) for i in range(2)]
        hb_cnt = [0]
        kuk = load_w(w_ukv[l, :, :], 128, 2, 768)
        agc = agc_out[l].ap()
        sc = float(96.0 ** -0.5)

        def expand(h):
            for kb in range(10):
                if kb == 1:
                    cs, kc = CKC, "CKC"
                else:
                    cs, kc = CST[kb % 2], ("CST", kb % 2)
                    if kb == 0:
                        P.dma(V(cs, kc), V(pkv[l, 0:256, :].rearrange("(c p) t -> p c t", p=128), ("PKV", l)))
                    else:
                        r, hf = divmod(kb - 2, 2)
                        P.dma(V(cs, kc), V(agc[r * 256:(r + 1) * 256, hf * 512:(hf + 1) * 512].rearrange("(c p) t -> p c t", p=128), ("AGCO", l)))
                b = P.psum("m")
                P.mm(ps(b, 64), [(wv(kuk, 128, c2, h * 192, 64), V(cs[:, c2, :], kc)) for c2 in range(2)])
                P.copy(V(KN[0:64, kb * 512:(kb + 1) * 512], "KN"), ps(b, 64), eng="act")
                b = P.psum("m")
                for t4 in range(4):
                    P.mm(ps(b, 128, t4 * 128, 128), [(V(cs[:, c2, t4 * 128:(t4 + 1) * 128], kc), wv(kuk, 128, c2, h * 192 + 64, 128)) for c2 in range(2)])
                P.copy(V(VC[:, kb * 4:(kb + 1) * 4, :], "VC"), V(PS[b][:, :].rearrange("p (t d) -> p t d", t=4), ("PS", b)))

        def add_head(items, h, blk, t0, n, tiles):
            ob = P.psum("a")
            db = P.psum("x")
            q1 = V(QC[:, h, t0:t0 + n], ("QC", h, blk))
            last = len(tiles) - 1

            def fin():
                P.recip(V(RC[:, 0:n], "RC"), ps(db, 128, 0, n))
                P.tt(V(TM[:, 0:n], "TM"), ps(ob, 128, 0, n), V(RC[:, 0:n], "RC"), ALU.mult)
                yc = V(YC[:, h, t0:t0 + n], ("YC", h, blk))
                P.tt(yc, V(TM[:, 0:n], "TM"), yc, ALU.mult, eng="pool")
            for i, kt in enumerate(tiles):
                def S(bank, kt=kt):
                    P.mm1(ps(bank, 128, 0, n), V(KN[:, kt * 128:(kt + 1) * 128], "KN"), q1, True, True)

                def E(bank, ring):
                    P.act(V(PT[ring][:, 0:n], ("PT", ring)), ps(bank, 128, 0, n), AF.Exp, scale=sc)

                def PV(ring, i=i, kt=kt):
                    pt = V(PT[ring][:, 0:n], ("PT", ring))
                    P.mm1(ps(ob, 128, 0, n), V(VC[:, kt, :], "VC"), pt, i == 0, i == last, inc=False)
                    P.mm1(ps(db, 128, 0, n), V(ONESB, "ONESB"), pt, i == 0, i == last, inc=(i == last))
                items.append((S, E, PV, fin if i == last else None))

        for h in range(4):
            expand(h)
            items = []
            for s_ in range(2):
                add_head(items, h, 0, s_ * 256, 256, [2 * s_, 2 * s_ + 1])
            for qb in range(2):
                add_head(items, h, 1 + qb, TP + qb * 512, 512, list(range(4, NKA // 128)))
            attention_stream(items)
        if MLA_CUT <= 5:
            return
        merge_branch(l, 2, lambda ch, blk: V(YC[:, ch, BLOCKS[blk][0]:BLOCKS[blk][0] + BLOCKS[blk][1]], ("YC", ch, blk)), 4, 128)


    def phase_gla(l):
        YB = AR.alloc("YB", 128, [4, T], BF16)
        yb_off = AR.off
        QG = AR.alloc("QG", 128, [2, T], BF16)
        KGT = AR.alloc("KGT", 128, [2, T], BF16)
        KGK = AR.alloc("KGK", 128, [NT, 256], BF16)
        VGK = AR.alloc("VGK", 128, [NT, 512], BF16)
        RFA = AR.alloc("RFA", 49, [T])
        OGB = AR.alloc("OGB", 128, [4, T], BF16)
        WD = AR.alloc("WD", 49, [256])
        MASK4 = [AR.alloc("MASK4%d" % d, 128, [4, 128], BF16) for d in range(2)]
        S32 = [AR.alloc("S32%d" % d, 128, [2, 128]) for d in range(2)]
        SB16 = [AR.alloc("SB16%d" % d, 128, [2, 128], BF16) for d in range(2)]
        PTOT = AR.alloc("PTOT", 128, [2, 2])
        SPB = [AR.alloc("SPB%d" % i, 128, [256]) for i in range(2)]
        EXB = [AR.alloc("EX%d" % i, 128, [256]) for i in range(2)]
        KL = [AR.alloc("KL%d" % i, 128, [256], BF16) for i in range(2)]
        E1B = [AR.alloc("E1%d" % i, 128, [2, 128]) for i in range(2)]
        E2B = [AR.alloc("E2%d" % i, 128, [2, 128]) for i in range(2)]
        QDB = [AR.alloc("QD%d" % i, 128, [2, 128], BF16) for i in range(2)]
        KDB = [AR.alloc("KD%d" % i, 128, [2, 128], BF16) for i in range(2)]
        ATMB = [AR.alloc("ATM%d" % i, 128, [4, 128], BF16) for i in range(2)]
        DC = [AR.alloc("DC%d" % i, 128, [2]) for i in range(2)]
        OGFB = [AR.alloc("OGF%d" % i, 128, [512]) for i in range(2)]
        SQ2B = [AR.alloc("SQ2%d" % i, 128, [512]) for i in range(2)]
        RS2B = [AR.alloc("RS2%d" % i, 128, [512]) for i in range(2)]
        A2 = AR.alloc("A2", 128, [2])
        LM = AR.alloc("LM", 128, [256])
        FBP = [AR.alloc("FBP%d" % i, 128, [258]) for i in range(2)]
        vS = lambda d: V(S32[d], ("S32", d))
        vSB = lambda d: V(SB16[d], ("SB16", d))
        TRI = [V(CONS[:, 1, :], "CONS"), V(CONS[:, 3, :], "CONS")]
        TRIC = [V(CONS[:, 2, :], "CONS"), V(CONS[:, 4, :], "CONS")]
        NEG1 = V(CONS[:, 1, 127:128], "CONS")

        P.memset(V(RFA, "RFA"), 1.0)
        for d in range(2):
            P.dma(V(WD[32 * d:32 * d + 17, :], "WD"), V(wdec[l, d, :, :], "d.wd"))
            for h in range(4):
                P.copy(V(MASK4[d][:, h, :], ("MASK4", d)), V(CONS[:, 5 + d, :], "CONS"), eng="pool")

        def mk_qk(is_q):
            o = O_QG if is_q else O_KG

            def load():
                return load_w(w_in[l, :, o:o + 256], 128, 8, 256)

            def comp(k):
                for pr in range(2):
                    for blk in range(3):
                        t0, n = BLOCKS[blk]
                        b = proj2(k, 128, 128 * pr, 128, blk)
                        if is_q:
                            P.act(V(QG[:, pr, t0:t0 + n], ("QG", blk)), ps(b), AF.Identity, scale=0.125)
                        else:
                            P.copy(V(KGT[:, pr, t0:t0 + n], ("KGT", blk)), ps(b), eng="act")
                if not is_q:
                    for t in range(NT):
                        b = P.psum("m")
                        P.mm(ps(b, 128, 0, 256), [(htv(c, t * 128, 128), wv(k, 128, c, 0, 256)) for c in range(8)])
                        P.copy(V(KGK[:, t, :], ("KGK", t)), ps(b, 128, 0, 256))
            return load, comp

        def load_vg():
            return load_w(w_in[l, :, O_VG:O_VG + 512], 128, 8, 512)

        def comp_vg(k):
            for t in range(NT):
                b = P.psum("m")
                P.mm(ps(b), [(htv(c, t * 128, 128), wv(k, 128, c, 0, 512)) for c in range(8)])
                if t % 2 == 0:
                    P.copy(V(VGK[:, t, :], ("VGK", t)), ps(b))
                else:
                    P.copy(V(VGK[:, t, :], ("VGK", t)), ps(b), eng="act")

        def mk_gg(i):
            def load():
                return load_w(w_in[l, :, O_GG + i * 256:O_GG + (i + 1) * 256], 128, 8, 256)

            def comp(k):
                for cc in range(2):
                    ch = 2 * i + cc
                    for blk in range(3):
                        t0, n = BLOCKS[blk]
                        b = proj2(k, 128, 128 * cc, 128, blk)
                        P.act(V(YB[:, ch, t0:t0 + n], ("YB", ch, blk)), ps(b), AF.Silu)
            return load, comp

        def load_r():
            return load_w(w_in[l, :, O_RF:O_RF + 32], 128, 8, 32)

        def comp_r(k):
            for d in range(2):
                for blk in range(3):
                    t0, n = BLOCKS[blk]
                    b = proj2(k, 128, 16 * d, 16, blk)
                    P.copy(V(RFA[32 * d:32 * d + 16, t0:t0 + n], "RFA"), ps(b, 16), eng="act")
        run_jobs([(load_r, comp_r), mk_qk(False), (load_vg, comp_vg), mk_qk(True), mk_gg(0), mk_gg(1)])

        if GLA_CUT <= 1:
            return
        def prep(t, d, need_dc=True):
            i = d
            sp = V(SPB[i], ("SPB", i))
            ex = V(EXB[i], ("EX", i))
            bx = P.psum("x")
            P.mm(ps(bx, 128, 0, 256), [(V(RFA[32 * d:32 * d + 17, t * 128:(t + 1) * 128], "RFA"), V(WD[32 * d:32 * d + 17, :], "WD"))])
            yield
            P.act(ex, ps(bx, 128, 0, 256), AF.Exp, scale=-1.0)
            P.act(sp, ex, AF.Ln, bias=1.0)
            yield
            br = P.psum("x")
            P.mm(ps(br, 128, 0, 256), [(TRIC[d], sp)])
            yield
            P.act(ex, ps(br, 128, 0, 256), AF.Exp)
            kl = V(KL[i], ("KL", i))
            P.tt(kl, V(KGK[:, t, :], ("KGK", t)), ex, ALU.mult)
            yield
            if not need_dc:
                return sp, kl, None
            bt_ = P.psum("x")
            for pr in range(2):
                P.mm(ps(bt_, 128, pr, 1), [(V(SPB[i][:, pr * 128:(pr + 1) * 128], ("SPB", i)), NEG1)])
            yield
            dc = V(DC[i], ("DC", i))
            P.act(dc, ps(bt_, 128, 0, 2), AF.Exp)
            yield
            return sp, kl, dc

        def state_update(t, d, kl, dc):
            bs_ = P.psum("m")
            for h in range(4):
                pr = h // 2
                P.mm(ps(bs_, 128, h * 128, 128), [(V(kl.ap[:, pr * 128:(pr + 1) * 128], kl.keys[0]), V(VGK[:, t, h * 128:(h + 1) * 128], ("VGK", t)))])
            yield
            for h in range(4):
                pr, hh = divmod(h, 2)
                rows = slice(64 * hh, 64 * hh + 64)
                sv = V(S32[d][rows, pr, :], ("S32", d))
                P.stt(sv, sv, V(dc.ap[rows, pr:pr + 1], dc.keys[0]), V(PS[bs_][rows, h * 128:(h + 1) * 128], ("PS", bs_)), ALU.mult, ALU.add)
            P.copy(vSB(d), vS(d), eng="act")
            yield

        def step_state_only(t, d):
            sp, kl, dc = yield from prep(t, d)
            yield from state_update(t, d, kl, dc)
            pt_ = V(PTOT[:, d, :], "PTOT")
            P.tt(pt_, pt_, dc, ALU.mult)
            yield

        def step_full(t, d, second):
            blk = 0 if t < 4 else 1 + (t - 4) // 4
            E1, E2, QD, KD, ATM = E1B[d], E2B[d], QDB[d], KDB[d], ATMB[d]
            kE1, kE2, kQD, kKD, kATM = ("E1", d), ("E2", d), ("QD", d), ("KD", d), ("ATM", d)
            sp, kl, _ = yield from prep(t, d, need_dc=False)
            last = 127 if d == 0 else 0
            dc = V(E1[:, :, last], kE1)
            bc = P.psum("x")
            for pr in range(2):
                P.mm(ps(bc, 128, pr * 128, 128), [(V(sp.ap[:, pr * 128:(pr + 1) * 128], sp.keys[0]), TRI[d])])
            yield
            cv = V(PS[bc][:, 0:256].rearrange("p (a t) -> p a t", a=2), ("PS", bc))
            P.act(V(E1, kE1), cv, AF.Exp)
            P.act(V(E2, kE2), cv, AF.Exp, scale=-1.0)
            yield
            P.tt(V(QD, kQD), V(QG[:, :, t * 128:(t + 1) * 128], ("QG", blk)), V(E1, kE1), ALU.mult)
            P.tt(V(KD, kKD), V(KGT[:, :, t * 128:(t + 1) * 128], ("KGT", blk)), V(E2, kE2), ALU.mult)
            yield
            bas = [P.psum("m"), P.psum("m")]
            for h in range(4):
                pr, hh = divmod(h, 2)
                rows = slice(64 * hh, 64 * hh + 64)
                P.mm(ps(bas[hh], 128, pr * 128, 128), [(V(KD[rows, pr, :], kKD), V(QD[rows, pr, :], kQD))])
            yield
            atm4 = ATM[:, :, :].rearrange("p (a b) t -> p a b t", b=2)
            msk4 = MASK4[d][:, 0:2, :]
            for hh in range(2):
                P.tt(V(atm4[:, :, hh, :], kATM), V(PS[bas[hh]][:, 0:256].rearrange("p (a t) -> p a t", a=2), ("PS", bas[hh])), V(msk4, ("MASK4", d)), ALU.mult)
            yield
            bo = P.psum("a")
            for h in range(4):
                pr, hh = divmod(h, 2)
                rows = slice(64 * hh, 64 * hh + 64)
                P.mm1(ps(bo, 128, h * 128, 128), V(VGK[:, t, h * 128:(h + 1) * 128], ("VGK", t)), V(ATM[:, h, :], kATM), True, False, inc=False)
                P.mm1(ps(bo, 128, h * 128, 128), V(SB16[d][rows, pr, :], ("SB16", d)), V(QD[rows, pr, :], kQD), False, True)
            yield
            ogb = V(OGB[:, :, t * 128:(t + 1) * 128], ("OGB", t))
            pso = V(PS[bo][:, :].rearrange("p (h t) -> p h t", h=4), ("PS", bo))
            if not second:
                P.copy(ogb, pso, eng="act")
                yield
            else:
                OGF, SQ2, RS2 = OGFB[d], SQ2B[d], RS2B[d]
                kO, kS, kR = ("OGF", d), ("SQ2", d), ("RS2", d)
                ogf = V(OGF[:, :].rearrange("p (h t) -> p h t", h=4), kO)
                P.tt(ogf, pso, ogb, ALU.add)
                yield
                P.tt(V(SQ2, kS), V(OGF, kO), V(OGF, kO), ALU.mult)
                yield
                bn = P.psum("x")
                P.mm(ps(bn), [(ONES(), V(SQ2, kS))])
                yield
                P.act(V(RS2, kR), ps(bn), AF.Ln, bias=EPS, scale=1.0 / 128)
                P.act(V(RS2, kR), V(RS2, kR), AF.Exp, scale=-0.5)
                yield
                P.stt(V(SQ2, kS), V(OGF, kO), small(4 + l), V(RS2, kR), ALU.mult, ALU.mult)
                yb = V(YB[:, :, t * 128:(t + 1) * 128], [("YB", ch, blk) for ch in range(4)])
                P.tt(yb, V(SQ2[:, :].rearrange("p (h t) -> p h t", h=4), kS), yb, ALU.mult, eng="pool")
                yield
            yield from state_update(t, d, kl, dc)

        def run_chains(chains):
            live = list(chains)
            while live:
                nxt = []
                for g_ in live:
                    try:
                        next(g_)
                        nxt.append(g_)
                    except StopIteration:
                        pass
                live = nxt

        def chain(tiles, d, full):
            n = len(tiles)
            for k in range(n):
                t = tiles[k] if d == 0 else tiles[n - 1 - k]
                if full:
                    yield from step_full(t, d, k >= n - 1 - k and not (k == n - 1 - k and d == 1))
                else:
                    yield from step_state_only(t, d)

        def zero_state(d):
            P.memset(vS(d), 0.0)
            P.memset(vSB(d), 0.0)

        SAMPLE = list(range(4, NT))
        P.memset(V(PTOT, "PTOT"), 1.0)
        agi = ag2_in[l].ap()
        for d in range(2):
            zero_state(d)
        run_chains([chain(SAMPLE, 0, False), chain(SAMPLE, 1, False)])
        for d in range(2):
            P.dma(V(agi[d * 128:(d + 1) * 128, 0:256], ("AG2", l)), V(S32[d][:, :, :].rearrange("p a v -> p (a v)"), ("S32", d)))
            P.dma(V(agi[d * 128:(d + 1) * 128, 256:258], ("AG2", l)), V(PTOT[:, d, :], "PTOT"))
        em.collective(cc_sems[3 * l + 2],
                      lambda e: e.collective_compute("AllGather", ALU.bypass, replica_groups=[[0, 1, 2, 3], [4, 5, 6, 7]],
                                                     ins=[ag2_in[l].ap().opt()], outs=[ag2_out[l].ap().opt()]),
                      reads=[("AG2", l)], writes=[("AG2O", l)])
        st_view = lambda o, sq: o[sq, l].rearrange("(pr hh) k v -> (hh k) pr v", hh=2)
        for sq in range(2):
            tiles = [2 * sq, 2 * sq + 1]
            zero_state(0)
            zero_state(1)
            run_chains([chain(tiles, 0, True), chain(tiles, 1, True)])
            P.dma(V(st_view(o_sb, sq), ("o.sb", l, sq)), vS(1))
            P.dma(V(st_view(o_sf, sq), ("o.sf", l, sq)), vS(0))
        ago2 = ag2_out[l].ap()
        fcnt = 0
        for d in range(2):
            src = (s_f, s_b)[d]
            P.dma(vS(d), V(src[l].rearrange("(pr hh) k v -> (hh k) pr v", hh=2), "d.st"))
            for r in (range(4) if d == 0 else range(3, -1, -1)):
                m_c = V(FOLD[:, d, r, 0:1], "FOLD")
                om_c = V(FOLD[:, d, r, 1:2], "FOLD")
                fb = FBP[fcnt % 2]
                kfb = ("FBP", fcnt % 2)
                fcnt += 1
                P.dma(V(fb, kfb), V(ago2[r * 256 + d * 128:r * 256 + (d + 1) * 128, :], ("AG2O", l)))
                P.ts(V(A2, "A2"), V(fb[:, 256:258], kfb), m_c, ALU.mult, om_c, ALU.add)
                P.ts(V(LM, "LM"), V(fb[:, 0:256], kfb), m_c, ALU.mult)
                for pr in range(2):
                    sv = V(S32[d][:, pr, :], ("S32", d))
                    P.stt(sv, sv, V(A2[:, pr:pr + 1], "A2"), V(LM[:, pr * 128:(pr + 1) * 128], "LM"), ALU.mult, ALU.add)
            P.copy(vSB(d), vS(d), eng="act")
        run_chains([chain(SAMPLE, 0, True), chain(SAMPLE, 1, True)])
        em.barrier()
        AR.off = yb_off
        merge_branch(l, 1, lambda ch, blk: V(YB[:, ch, BLOCKS[blk][0]:BLOCKS[blk][0] + BLOCKS[blk][1]], ("YB", ch, blk)), 4, 128)


    def phase_out(l):
        src = xin if l == 0 else xs
        dst = xs if l == 0 else y
        XT = [AR.alloc("XT%d" % i, 128, [D]) for i in range(3)]
        O32 = [AR.alloc("O32%d" % i, 128, [D]) for i in range(3)]
        JK = AR.alloc("JK", 128, [512], BF16)
        ST = AR.alloc("STO", 128, [NT, 4])
        P.memset(V(ST, "STO"), 0.0)
        kh = [load_w(w_out[l, :, hf * 512:(hf + 1) * 512], 128, 8, 512) for hf in range(2)]
        for t in range(NT):
            g = 0 if t < 4 else 1
            i = t % 3
            xt = V(XT[i], ("XT", i))
            o32 = V(O32[i], ("O32", i))
            P.dma(xt, V(src[t * 128:(t + 1) * 128, :], ("xs", t)))
            for hf in range(2):
                b = P.psum("m")
                P.mm(ps(b), [(V(MERGED[:, c, t * 128:(t + 1) * 128], "MERGED"), wv(kh[hf], 128, c, 0, 512)) for c in range(8)])
                oh = V(O32[i][:, hf * 512:(hf + 1) * 512], ("O32", i))
                P.copy(oh, ps(b), eng=("act" if hf == 0 else "dve"))
                P.act(V(JK, "JK"), oh, AF.Square, accum=V(ST[:, t, hf:hf + 1], "STO"))
            P.tt(V(ST[:, t, 2:3], "STO"), V(ST[:, t, 0:1], "STO"), V(ST[:, t, 1:2], "STO"), ALU.add)
            P.act(V(ST[:, t, 3:4], "STO"), V(ST[:, t, 2:3], "STO"), AF.Sqrt, bias=EPS, scale=1.0 / D)
            P.recip(V(ST[:, t, 3:4], "STO"), V(ST[:, t, 3:4], "STO"))
            P.stt(o32, o32, V(ST[:, t, 3:4], "STO"), V(GB[:, l, g, :], "GB"), ALU.mult, ALU.mult)
            P.tt(o32, o32, xt, ALU.add)
            P.dma(V(dst[t * 128:(t + 1) * 128, :], ("xs", t) if l == 0 else ("y", t)), o32)

    KTP = None
    VAP = None

    phase_mod()
    for l in range(L):
        mstate['first'] = True
        new_phase()
        phase_ht(l)
        new_phase()
        KTP = AR.alloc("KTP", 128, [TP], BF16)
        VAP = AR.alloc("VAP", 128, [4, 2, 128], BF16)
        P.memset(V(VAP, "VAP"), 1.0)
        QT = AR.alloc("QT", 128, [4, T], BF16)
        YA = AR.alloc("YA", 128, [4, T], BF16)
        layer_base = AR.off
        phase_kv(l, QT, YA)
        if STAGE <= 1:
            break
        new_phase(layer_base)
        if not SKIP_GQA:
            phase_gqa(l, QT, YA)
        if STAGE <= 2:
            em.barrier()
            P.em.dma("sp", dbg[:, :, :], MERGED[:], reads=[], writes=["dbg"])
            break
        new_phase(0)
        if not SKIP_MLA:
            phase_mla(l)
        if STAGE <= 3:
            em.barrier()
            P.em.dma("sp", dbg[:, :, :], MERGED[:], reads=[], writes=["dbg"])
            break
        new_phase(0)
        phase_gla(l)
        if STAGE <= 4:
            em.barrier()
            P.em.dma("sp", dbg[:, :, :], MERGED[:], reads=[], writes=["dbg"])
            break
        new_phase(0)
        phase_out(l)
        if STAGE <= 5:
            break
    em.barrier()
    em.finish("sp")
    em.replay(block)
    P.st.close()
    return nc


_CACHE = {}


def _consts():
    c = np.zeros((128, 12, 128), np.float32)
    i = np.arange(128)
    c[:, 0, :] = np.eye(128)
    s = -1.0 / 16.0
    c[:, 1, :] = s * (i[:, None] <= i[None, :])
    c[:, 2, :] = s * (i[:, None] > i[None, :])
    c[:, 3, :] = s * (i[:, None] >= i[None, :])
    c[:, 4, :] = s * (i[:, None] < i[None, :])
    c[:, 5, :] = (i[:, None] <= i[None, :])
    c[:, 6, :] = (i[:, None] >= i[None, :])
    c[:, 7, :] = 1.0
    c[0:64, 8, 0:64] = perm_matrix(64)
    c[0:32, 9, 0:32] = perm_matrix(32)
    c[0:64, 10, 0:64] = 1.0
    c[64:128, 10, 64:128] = 1.0
    c[0:64, 11, 0:64] = perm_matrix(64)
    c[64:128, 11, 64:128] = perm_matrix(64)
    return c


def kernel(**inp):
    f = lambda a: np.ascontiguousarray(np.asarray(a, dtype=np.float32))
    x_prompt, x_sample = f(inp["x_prompt"]), f(inp["x_sample"])
    if "nc" not in _CACHE:
        _CACHE["nc"] = build_program()
    nc = _CACHE["nc"]
    col = lambda v, p: np.ascontiguousarray(v.reshape(v.shape[0], -1, p).transpose(2, 0, 1))
    wdec = np.stack([np.concatenate([f(inp["w_gla_decay_fwd"]), f(inp["b_gla_decay_fwd"])[:, None, :]], 1),
                     np.concatenate([f(inp["w_gla_decay_bwd"]), f(inp["b_gla_decay_bwd"])[:, None, :]], 1)], 1)
    shared = {
        "b_mod": f(inp["b_mod"]), "g_post": f(inp["g_post"]),
        "g_pre_c": col(f(inp["g_pre"]), 128), "w_in": f(inp["w_in"]),
        "gqk_c": np.ascontiguousarray(np.stack([f(inp["g_q_norm"]), f(inp["g_k_norm"])], -1).transpose(1, 0, 2)),
        "wdec": np.ascontiguousarray(wdec),
        "g_gla_c": np.ascontiguousarray(f(inp["g_gla_out"]).T),
        "g_mq_c": col(f(inp["g_mla_q"]), 128), "g_mkv_c": col(f(inp["g_mla_kv"]), 128),
        "w_uq": f(inp["w_mla_uq"]), "w_ukv": f(inp["w_mla_ukv"]),
        "w_o_gqa": f(inp["w_o_gqa"]), "w_o_gla": f(inp["w_o_gla"]), "w_o_mla": f(inp["w_o_mla"]),
        "w_out": f(inp["w_out"]), "consts": _consts(),
    }
    c, c_ctx = f(inp["c"]), f(inp["c_ctx"])
    w_mod_full = f(inp["w_mod"])
    condT3 = np.ascontiguousarray(np.stack([c_ctx, c[0], c[1]], 0).reshape(3, 8, 128).transpose(2, 1, 0))
    in_maps = []
    for core in range(NCORES):
        b, q = divmod(core, 4)
        m = dict(shared)
        m["xin"] = np.concatenate([x_prompt[2 * core].reshape(256, D), x_prompt[2 * core + 1].reshape(256, D),
                                   x_sample[b, q * TS:(q + 1) * TS]], 0)
        m["condT"] = condT3
        m["w_mod"] = np.ascontiguousarray(w_mod_full[:, :, q * 768:(q + 1) * 768])
        sel = np.zeros((4, 2 + 256), np.float32)
        sel[0, 0] = sel[1 + b, 1] = sel[3, 0] = sel[3, 1] = 1.0
        sel[0, 2:130] = 1.0
        sel[1 + b, 130:258] = 1.0
        sel[3, 2:258] = 1.0
        m["sel3"] = sel
        m["c_k"] = f(inp["cache_gqa_k"])[b].reshape(L, PAST, 128)
        m["c_v"] = f(inp["cache_gqa_v"])[b].reshape(L, PAST, 128)
        m["c_ckv"] = f(inp["cache_mla_ckv"])[b]
        m["c_kr"] = f(inp["cache_mla_krope"])[b]
        m["s_f"] = f(inp["state_gla_fwd"])[b]
        m["s_b"] = f(inp["state_gla_bwd"])[b]
        c64, s64, c32, s32 = rope_tables(core)
        m["rope64"] = np.ascontiguousarray(np.stack([c64, s64], 1))
        m["rope32"] = np.ascontiguousarray(np.stack([c32, s32], 1))
        fm = np.zeros((128, 2, 4, 2), np.float32)
        for r in range(4):
            fm[:, 0, r, 0] = 1.0 if r < q else 0.0
            fm[:, 1, r, 0] = 1.0 if r > q else 0.0
        fm[..., 1] = 1.0 - fm[..., 0]
        m["foldm"] = fm
        in_maps.append(m)
    res = run_bass_kernel_spmd(nc, in_maps, core_ids=list(range(NCORES)))
    R = res.results
    y_prompt = np.stack([R[i // 2]["y"][(i % 2) * 256:(i % 2 + 1) * 256] for i in range(16)], 0)
    y_sample = np.stack([np.concatenate([R[b * 4 + q]["y"][TP:] for q in range(4)], 0) for b in range(2)], 0)
    cat = lambda k: np.concatenate([R[i][k] for i in range(NCORES)], 0)
    return (y_prompt.astype(np.float32), y_sample.astype(np.float32),
            cat("o_k").reshape(16, L, 256, 2, 64), cat("o_v").reshape(16, L, 256, 2, 64),
            cat("o_ckv"), cat("o_kr"), cat("o_sf"), cat("o_sb"))
```

```python
import contextlib
import numpy as np
import concourse.bass as bass
import concourse.mybir as mybir
from concourse.bass_utils import run_bass_kernel_spmd

F32 = mybir.dt.float32
BF16 = mybir.dt.bfloat16
ALU = mybir.AluOpType
AF = mybir.ActivationFunctionType

NCORES = 8
L = 2
D = 1024
NIN = 6976
TP = 512
TS = 1024
T = TP + TS
NT = T // 128
PAST = 512
NKS = PAST + 4096
EPS = 1e-6
O_QA, O_KA, O_VA, O_GA = 0, 512, 640, 768
O_QG, O_KG, O_VG, O_GG = 1280, 1536, 1792, 2304
O_RF, O_RB = 2816, 2832
O_QL, O_KV, O_KR, O_GC = 2848, 3104, 3360, 3392
O_M1, O_M2, O_M3 = 3904, 4928, 5952
BLOCKS = [(0, 512), (512, 512), (1024, 512)]
AGR = 544

CC_ASYNC = False
STAGE = 99
MLA_CUT = 99
SKIP_GQA = False
GLA_CUT = 99
FULL_CUT = 99
SKIP_MLA = False
MLA_JOBS = 4


class Emitter:
    ENGS = ("pe", "act", "dve", "pool", "sp")

    def __init__(self, n_dma_sems=20):
        self.ops = {e: [] for e in self.ENGS}
        self.cnt = {e: 0 for e in self.ENGS}
        self.sem = {}
        self.res = {}
        self.waited = {e: {} for e in self.ENGS}
        self.n_dma_sems = n_dma_sems
        self.dma_rr = 0
        self.dma_rr2 = [0, 0]
        self.all_tokens = {}

    def sems_needed(self):
        return len(self.ENGS) + self.n_dma_sems

    def bind_sems(self, sems):
        for i, e in enumerate(self.ENGS):
            self.sem[e] = sems[i]
        self.dma_sems = list(sems[len(self.ENGS):len(self.ENGS) + self.n_dma_sems])
        self.dma_cnt = [0] * len(self.dma_sems)

    def _need(self, eng, tokens):
        out = []
        for (s, v, owner) in tokens:
            if owner == eng and eng == "pe":
                continue
            w = self.waited[eng]
            if w.get(id(s), 0) >= v:
                continue
            w[id(s)] = v
            out.append((s, v))
        return out

    def _deps(self, eng, reads, writes):
        toks = []
        for r in reads:
            st = self.res.get(r)
            if not st:
                continue
            if st["w"] is not None:
                toks.append(st["w"])
            if isinstance(r, tuple) and r[0] == "PS":
                toks.extend(t for t in st["r"].values() if t[2] != eng)
        for w in writes:
            st = self.res.get(w)
            if st:
                if st["w"] is not None:
                    toks.append(st["w"])
                toks.extend(st["r"].values())
        return self._need(eng, toks)

    def _commit(self, tok, reads, writes):
        self.all_tokens[id(tok[0])] = max(self.all_tokens.get(id(tok[0]), (None, 0, None)), tok, key=lambda t: t[1])
        for r in reads:
            st = self.res.setdefault(r, {"w": None, "r": {}})
            st["r"][(tok[2], id(tok[0]))] = tok
        for w in writes:
            self.res[w] = {"w": tok, "r": {}}

    def op(self, eng, fn, reads=(), writes=(), inc=True):
        waits = self._deps(eng, reads, writes)
        sem = self.sem[eng]
        if inc:
            self.cnt[eng] += 1
            val = self.cnt[eng]
        else:
            val = self.cnt[eng] + 1
        self._commit((sem, val, eng), reads, writes)

        def run(e, waits=waits, fn=fn, inc=inc, sem=sem):
            for (s, v) in waits:
                e.wait_ge(s, v)
            ins = fn(e)
            if inc:
                ins.then_inc(sem, 1)
        self.ops[eng].append(run)

    def dma(self, eng, out, in_, reads=(), writes=(), **kw):
        half = len(self.dma_sems) // 2
        q = 1 if eng == "pool" else 0
        k = q * half + self.dma_rr2[q]
        self.dma_rr2[q] = (self.dma_rr2[q] + 1) % half
        s = self.dma_sems[k]
        prev = self.dma_cnt[k]
        self.dma_cnt[k] += 1
        toks = [(s, 16 * prev, "dma")] if prev else []
        waits = self._need(eng, toks) + self._deps(eng, reads, writes)
        self._commit((s, 16 * self.dma_cnt[k], "dma"), reads, writes)

        def run(e, waits=waits, s=s):
            for (ss, v) in waits:
                e.wait_ge(ss, v)
            e.dma_start(out=out, in_=in_, **kw).then_inc(s, 16)
        self.ops[eng].append(run)

    def collective(self, sem, fn, reads=(), writes=()):
        waits = self._deps("pool", reads, writes)
        self._commit((sem, 1, "cc"), reads, writes)

        def run(e, waits=waits):
            for (ss, v) in waits:
                e.wait_ge(ss, v)
            fn(e).then_inc(sem)
        self.ops["pool"].append(run)

    def barrier(self):
        toks = [t for t in self.all_tokens.values() if (t[2] != "cc" or not CC_ASYNC)]
        for eng in self.ENGS:
            waits = self._need(eng, toks)

            def run(e, waits=waits):
                for (s, v) in waits:
                    e.wait_ge(s, v)
            self.ops[eng].append(run)
        self.res = {k: {"w": v["w"], "r": {}} for k, v in self.res.items() if v["w"] is not None and v["w"][2] == "cc"}

    def finish(self, eng="sp"):
        toks = list(self.all_tokens.values())
        waits = self._need(eng, toks)

        def run(e, waits=waits):
            for (s, v) in waits:
                e.wait_ge(s, v)
        self.ops[eng].append(run)

    def replay(self, block):
        ops = self.ops

        @block.tensor
        def _(e):
            for f in ops["pe"]:
                f(e)

        @block.scalar
        def _(e):
            for f in ops["act"]:
                f(e)

        @block.vector
        def _(e):
            for f in ops["dve"]:
                f(e)

        @block.gpsimd
        def _(e):
            for f in ops["pool"]:
                f(e)

        @block.sync
        def _(e):
            for f in ops["sp"]:
                f(e)


class V:
    __slots__ = ("ap", "keys")

    def __init__(self, ap, keys):
        self.ap = ap
        self.keys = list(keys) if isinstance(keys, list) else [keys]


def _keys(*vs):
    out = []
    for v in vs:
        if v is None or isinstance(v, (int, float)):
            continue
        out.extend(v.keys)
    return out


def _ap(x):
    return x.ap if isinstance(x, V) else x


class Prog:
    def __init__(self):
        self.nc = bass.Bass("TRN2", target_bir_lowering=False)
        self.em = Emitter()
        self.st = contextlib.ExitStack()
        self.ps_rr = {"m": 0, "a": 0, "x": 0}
        self.cc_sems = []

    def din(self, name, shape, dt=F32):
        return self.nc.dram_tensor(name, list(shape), dt, kind="ExternalInput").ap()

    def dout(self, name, shape, dt=F32):
        return self.nc.dram_tensor(name, list(shape), dt, kind="ExternalOutput").ap()

    def dscr(self, name, shape, dt=F32):
        return self.nc.dram_tensor(name, list(shape), dt)

    def sb(self, name, shape, dt=F32):
        return self.st.enter_context(self.nc.sbuf_tensor(name, list(shape), dt))

    def mm(self, out, pairs, fp32_ok=True):
        n = len(pairs)
        for i, (lhsT, rhs) in enumerate(pairs):
            self.em.op("pe", lambda e, o=out.ap, a=lhsT.ap, b=rhs.ap, i=i: e.matmul(o, lhsT=a, rhs=b, start=(i == 0), stop=(i == n - 1)),
                       reads=_keys(lhsT, rhs), writes=_keys(out), inc=(i == n - 1))

    def mm1(self, out, lhsT, rhs, start, stop, inc=True):
        self.em.op("pe", lambda e: e.matmul(out.ap, lhsT=lhsT.ap, rhs=rhs.ap, start=start, stop=stop),
                   reads=_keys(lhsT, rhs), writes=_keys(out), inc=inc)

    def transpose(self, out, in_, ident):
        self.em.op("pe", lambda e: e.transpose(out.ap, in_.ap, ident.ap), reads=_keys(in_, ident), writes=_keys(out))

    def act(self, out, in_, func, bias=None, scale=None, accum=None, eng="act"):
        kw = {}
        if bias is not None:
            kw["bias"] = _ap(bias)
        if scale is not None:
            kw["scale"] = _ap(scale)
        if accum is not None:
            kw["accum_out"] = accum.ap
        self.em.op(eng, lambda e: e.activation(out=out.ap, in_=in_.ap, func=func, **kw),
                   reads=_keys(in_, bias, scale), writes=_keys(out, accum))

    def tt(self, out, in0, in1, op, eng="dve"):
        self.em.op(eng, lambda e: e.tensor_tensor(out=out.ap, in0=in0.ap, in1=in1.ap, op=op),
                   reads=_keys(in0, in1), writes=_keys(out))

    def ts(self, out, in0, s1, op0, s2=None, op1=None, eng="dve"):
        if op1 is None:
            self.em.op(eng, lambda e: e.tensor_scalar(out=out.ap, in0=in0.ap, scalar1=_ap(s1), scalar2=None, op0=op0),
                       reads=_keys(in0, s1), writes=_keys(out))
        else:
            self.em.op(eng, lambda e: e.tensor_scalar(out=out.ap, in0=in0.ap, scalar1=_ap(s1), scalar2=_ap(s2), op0=op0, op1=op1),
                       reads=_keys(in0, s1, s2), writes=_keys(out))

    def stt(self, out, in0, scalar, in1, op0, op1, eng="dve"):
        self.em.op(eng, lambda e: e.scalar_tensor_tensor(out=out.ap, in0=in0.ap, scalar=_ap(scalar), in1=in1.ap, op0=op0, op1=op1),
                   reads=_keys(in0, scalar, in1), writes=_keys(out))

    def copy(self, out, in_, eng="dve"):
        if eng == "act":
            self.em.op("act", lambda e: e.copy(out=out.ap, in_=in_.ap), reads=_keys(in_), writes=_keys(out))
        else:
            self.em.op(eng, lambda e: e.tensor_copy(out=out.ap, in_=in_.ap), reads=_keys(in_), writes=_keys(out))

    def recip(self, out, in_):
        self.em.op("dve", lambda e: e.reciprocal(out=out.ap, in_=in_.ap), reads=_keys(in_), writes=_keys(out))

    def memset(self, out, val, eng="pool"):
        self.em.op(eng, lambda e: e.memset(out.ap, val), writes=_keys(out))

    def dma(self, out, in_, eng="sp", **kw):
        self.em.dma(eng, out.ap, in_.ap, reads=_keys(in_), writes=_keys(out), **kw)

    def psum(self, cls):
        lo, n = {"m": (0, 4), "a": (4, 2), "x": (6, 2)}[cls]
        i = lo + self.ps_rr[cls] % n
        self.ps_rr[cls] += 1
        return i


def rope_tables(core):
    q = core % 4
    t = np.arange(q * TS, (q + 1) * TS)
    row = (t // 64).astype(np.float32)
    col = (t % 64).astype(np.float32)

    def tab(half_dims):
        fr = (np.float32(10000.0) ** (-np.arange(half_dims, dtype=np.float32) / np.float32(half_dims))).astype(np.float32)
        ar = (row[None, :] * fr[:, None]).astype(np.float32)
        ac = (col[None, :] * fr[:, None]).astype(np.float32)
        cos = np.concatenate([np.cos(ar), np.cos(ar), np.cos(ac), np.cos(ac)], 0)
        sin = np.concatenate([-np.sin(ar), np.sin(ar), -np.sin(ac), np.sin(ac)], 0)
        return cos.astype(np.float32), sin.astype(np.float32)

    c64, s64 = tab(16)
    c32, s32 = tab(8)
    return c64, s64, c32, s32


def perm_matrix(n):
    h = n // 4
    p = np.zeros((n, n), np.float32)
    for m in range(n):
        blk, r = divmod(m, 2 * h)
        sw = blk * 2 * h + (r + h) % (2 * h)
        p[sw, m] = 1.0
    return p


def build_program():
    P = Prog()
    nc, em = P.nc, P.em
    xin = P.din("xin", [T, D])
    condT = P.din("condT", [128, 8, 3])
    w_mod = P.din("w_mod", [L, D, 768])
    b_mod = P.din("b_mod", [L, 3 * D])
    g_post = P.din("g_post", [L, D])
    g_pre_c = P.din("g_pre_c", [128, L, 8])
    w_in = P.din("w_in", [L, D, NIN])
    gqk_c = P.din("gqk_c", [64, L, 2])
    wdec = P.din("wdec", [L, 2, 17, 256])
    g_gla_c = P.din("g_gla_c", [128, L])
    g_mq_c = P.din("g_mq_c", [128, L, 2])
    g_mkv_c = P.din("g_mkv_c", [128, L, 2])
    w_uq = P.din("w_uq", [L, 256, 384])
    w_ukv = P.din("w_ukv", [L, 256, 768])
    w_o = [P.din("w_o_gqa", [L, 512, D]), P.din("w_o_gla", [L, 512, D]), P.din("w_o_mla", [L, 512, D])]
    w_out = P.din("w_out", [L, D, D])
    c_k = P.din("c_k", [L, PAST, 128])
    c_v = P.din("c_v", [L, PAST, 128])
    c_ckv = P.din("c_ckv", [L, PAST, 256])
    c_kr = P.din("c_kr", [L, PAST, 32])
    s_f = P.din("s_f", [L, 4, 64, 128])
    s_b = P.din("s_b", [L, 4, 64, 128])
    consts = P.din("consts", [128, 12, 128])
    rope64 = P.din("rope64", [64, 2, TS])
    rope32 = P.din("rope32", [32, 2, TS])
    sel3 = P.din("sel3", [4, 2 + 2 * 128])
    foldm = P.din("foldm", [128, 2, 4, 2])

    y = P.dout("y", [T, D])
    o_k = P.dout("o_k", [2, L, 256, 128])
    o_v = P.dout("o_v", [2, L, 256, 128])
    o_ckv = P.dout("o_ckv", [2, L, 256, 256])
    o_kr = P.dout("o_kr", [2, L, 256, 32])
    o_sf = P.dout("o_sf", [2, L, 4, 64, 128])
    o_sb = P.dout("o_sb", [2, L, 4, 64, 128])

    dbg = None
    xs = P.dscr("xs", [T, D]).ap()
    ag1_in = [P.dscr("ag1_in%d" % l, [288, TS], BF16) for l in range(L)]
    ag1_out = [P.dscr("ag1_out%d" % l, [4 * 288, TS], BF16) for l in range(L)]
    agc_in = [P.dscr("agc_in%d" % l, [256, TS], BF16) for l in range(L)]
    agc_out = [P.dscr("agc_out%d" % l, [4 * 256, TS], BF16) for l in range(L)]
    ag2_in = [P.dscr("ag2_in%d" % l, [256, 258]) for l in range(L)]
    ag2_out = [P.dscr("ag2_out%d" % l, [4 * 256, 258]) for l in range(L)]
    pkv = P.dscr("pkv", [L, 288, TP], BF16).ap()
    agm_in = P.dscr("agm_in", [2 * 3, 768])
    agm_out = P.dscr("agm_out", [4 * 2 * 3, 768])

    CONS = P.sb("CONS", [128, 12, 128])
    R64 = P.sb("R64", [128, 2, TS])
    R32 = P.sb("R32", [32, 2, TS])
    SEL3 = P.sb("SEL3", [4, 2 + 256])
    FOLD = P.sb("FOLD", [128, 2, 4, 2])
    SMALL = P.sb("SMALL", [128, 64])
    HT = P.sb("HT", [128, 8, T], BF16)
    MERGED = P.sb("MERGED", [128, 8, T], BF16)
    GB = P.sb("GB", [128, L, 2, D])
    ACBC = P.sb("ACBC", [128, L, 2, 8, 2])
    WB = [P.sb("WB%d" % i, [128, 4096], BF16) for i in range(3)]
    ARENA_BYTES = 96 * 1024
    ARENA = P.sb("ARENA", [128, ARENA_BYTES // 4])
    PS2 = [P.st.enter_context(nc.psum_tensor("ps%d" % i, [128, 1024], F32)) for i in range(4)]
    PS = [PS2[i // 2][:, (i % 2) * 512:(i % 2 + 1) * 512] for i in range(8)]
    sems = [P.st.enter_context(nc.semaphore("s%d" % i)) for i in range(em.sems_needed())]
    em.bind_sems(sems)
    cc_sems = [P.st.enter_context(nc.semaphore("cc%d" % i)) for i in range(3 * L + 1)]
    block = P.st.enter_context(nc.Block())

    def ps(i, p=128, lo=0, n=512):
        return V(PS[i][0:p, lo:lo + n], ("PS", i))

    IDENT = V(CONS[:, 0, :], "CONS")
    ONES = lambda p=128, n=128: V(CONS[0:p, 7, 0:n], "CONS")
    PERM64 = V(CONS[0:64, 8, 0:64], "CONS")
    PERM32 = V(CONS[0:32, 9, 0:32], "CONS")
    BDONES = V(CONS[:, 10, :], "CONS")
    BDPERM = V(CONS[:, 11, :], "CONS")

    class Arena:
        def __init__(self):
            self.off = 0

        def reset(self):
            self.off = 0

        def alloc(self, name, p, shape, dt=F32):
            n = int(np.prod(shape))
            words = n if dt == F32 else (n + 1) // 2
            a = ARENA[0:p, self.off:self.off + words]
            self.off += words
            assert self.off * 4 <= ARENA_BYTES, (name, self.off * 4)
            if dt != F32:
                a = a.bitcast(dt)
                a = a[:, 0:n]
            if len(shape) > 1:
                names = " ".join("d%d" % i for i in range(len(shape)))
                a = a.rearrange("p (%s) -> p %s" % (names, names), **{"d%d" % i: shape[i] for i in range(1, len(shape))})
            return a

    AR = Arena()

    def new_phase(keep=0):
        em.barrier()
        AR.off = keep

    SC0 = AR.alloc("SC", 128, [8, 3])
    P.dma(V(SC0, "SC"), V(condT[:, :, :], "d.c"))
    P.dma(V(SEL3[:], "SEL3"), V(sel3[:, :], "d.sel3"))
    P.dma(V(CONS[:], "CONS"), V(consts[:, :, :], "d.consts"))
    P.dma(V(R64[0:64], "R64"), V(rope64[:, :, :], "d.r64"))
    P.dma(V(R64[64:128], "R64"), V(rope64[:, :, :], "d.r64"))
    P.dma(V(R32[:], "R32"), V(rope32[:, :, :], "d.r32"))
    P.dma(V(FOLD[:], "FOLD"), V(foldm[:, :, :, :], "d.fold"))
    P.dma(V(SMALL[0:64, 0:4], "SMALL"), V(gqk_c.rearrange("p l g -> p (l g)"), "d.s"))
    P.dma(V(SMALL[64:128, 0:4], "SMALL"), V(gqk_c.rearrange("p l g -> p (l g)"), "d.s"))
    P.dma(V(SMALL[:, 4:6], "SMALL"), V(g_gla_c[:, :], "d.s"))
    P.dma(V(SMALL[:, 8:12], "SMALL"), V(g_mq_c.rearrange("p l g -> p (l g)"), "d.s"))
    P.dma(V(SMALL[:, 12:16], "SMALL"), V(g_mkv_c.rearrange("p l g -> p (l g)"), "d.s"))
    P.dma(V(SMALL[:, 16:32], "SMALL"), V(g_pre_c.rearrange("p l c -> p (l c)"), "d.s"))

    def small(col, p=128):
        return V(SMALL[0:p, col:col + 1], "SMALL")

    wstate = {"i": 0}

    def load_w(src, prows, nchunk, ncols):
        k = wstate["i"] % 3
        wstate["i"] += 1
        assert nchunk * ncols <= 4096
        dst = WB[k][0:prows, 0:nchunk * ncols].rearrange("p (c n) -> p c n", n=ncols)
        P.em.dma("pool", dst, src.rearrange("(c p) n -> p c n", p=prows), reads=[], writes=[("WB", k)])
        return (k, ncols)

    def wv(kh, prows, c, lo, n):
        k, ncols = kh
        return V(WB[k][0:prows, c * ncols + lo:c * ncols + lo + n], ("WB", k))

    def htv(c, t0, n):
        return V(HT[:, c, t0:t0 + n], [("HT", t) for t in range(t0 // 128, (t0 + n + 127) // 128)])

    def run_jobs(jobs):
        handles = [None] * len(jobs)
        for j in range(min(2, len(jobs))):
            handles[j] = jobs[j][0]()
        for i, (_, comp) in enumerate(jobs):
            if i + 2 < len(jobs):
                handles[i + 2] = jobs[i + 2][0]()
            comp(handles[i])

    def phase_mod():
        SC = SC0
        SCB = AR.alloc("SCB", 128, [8, 3], BF16)
        MR = AR.alloc("MR", 4, [3 * D])
        WM = [AR.alloc("WM%d" % i, 128, [8, 768], BF16) for i in range(2)]
        STG = AR.alloc("STG", 3, [2, 768])
        GPB = AR.alloc("GPB", 128, [D])
        P.act(V(SCB, "SCB"), V(SC, "SC"), AF.Silu)
        for l in range(L):
            P.dma(V(WM[l], ("WM", l)), V(w_mod[l, :, :].rearrange("(c p) n -> p c n", p=128), "d.wm"), eng="pool")
            for hf in range(2):
                b = P.psum("x")
                P.mm(ps(b, 3, 0, 384), [(V(SCB[:, c, :], "SCB"), V(WM[l][:, c, hf * 384:(hf + 1) * 384], ("WM", l))) for c in range(8)])
                P.copy(V(STG[:, l, hf * 384:(hf + 1) * 384], "STG"), ps(b, 3, 0, 384))
        P.dma(V(agm_in.ap().rearrange("(l g) c -> g l c", l=2), "AGM"), V(STG, "STG"))
        em.collective(cc_sems[3 * L],
                      lambda e: e.collective_compute("AllGather", ALU.bypass, replica_groups=[[0, 1, 2, 3], [4, 5, 6, 7]],
                                                     ins=[agm_in.ap().opt()], outs=[agm_out.ap().opt()]),
                      reads=["AGM"], writes=["AGMO"])
        gath = agm_out.ap().rearrange("(r l g) c -> l g r c", r=4, l=2, g=3)
        for l in range(L):
            P.dma(V(MR[0:3, :].rearrange("g (r c) -> g r c", r=4), "MR"), V(gath[l], "AGMO"))
            P.dma(V(MR[3:4, :], "MR"), V(b_mod[l:l + 1, :], "d.bm"))
            b = P.psum("x")
            for j in range(16):
                P.mm(ps(b, 128, 2 * j, 2), [(V(MR[0:4, j * 128:(j + 1) * 128], "MR"), V(SEL3[0:4, 0:2], "SEL3"))])
            pv = PS[b][:, 0:32].rearrange("p (j g) -> p j g", g=2)
            P.copy(V(ACBC[:, l, 1, :, :], "ACBC"), V(pv[:, 0:8, :], ("PS", b)))
            for g in range(2):
                P.stt(V(ACBC[:, l, 0, :, g], "ACBC"), V(pv[:, 8:16, g], ("PS", b)), 1.0, V(SMALL[:, 16 + 8 * l:24 + 8 * l], "SMALL"), ALU.add, ALU.mult)
            P.dma(V(GPB, "GPB"), V(g_post[l:l + 1, :].partition_broadcast(128), "d.gp"))
            for g in range(2):
                for hf in range(2):
                    b2 = P.psum("x")
                    P.mm(ps(b2), [(V(SEL3[0:4, 2 + 128 * g:2 + 128 * (g + 1)], "SEL3"), V(MR[0:4, 2 * D + hf * 512:2 * D + (hf + 1) * 512], "MR"))])
                    P.tt(V(GB[:, l, g, hf * 512:(hf + 1) * 512], "GB"), ps(b2), V(GPB[:, hf * 512:(hf + 1) * 512], "GPB"), ALU.mult)

    def phase_ht(l):
        src = xin if l == 0 else xs
        XB = [AR.alloc("XB%d" % i, 128, [D]) for i in range(3)]
        JUNK = AR.alloc("JUNK", 128, [D], BF16)
        ST = AR.alloc("ST", 128, [NT, 2])
        P.memset(V(ST, [("ST", t) for t in range(NT)]), 0.0)
        def stage_a(t):
            xb = XB[t % 3]
            kx = ("XB", t % 3)
            P.dma(V(xb, kx), V(src[t * 128:(t + 1) * 128, :], "d.x"))
            P.act(V(JUNK, "JUNK"), V(xb, kx), AF.Square, accum=V(ST[:, t, 0:1], ("ST", t)))
            P.act(V(ST[:, t, 1:2], ("ST", t)), V(ST[:, t, 0:1], ("ST", t)), AF.Sqrt, bias=EPS, scale=1.0 / D)
            P.recip(V(ST[:, t, 1:2], ("ST", t)), V(ST[:, t, 1:2], ("ST", t)))
            P.ts(V(xb, kx), V(xb, kx), V(ST[:, t, 1:2], ("ST", t)), ALU.mult)

        def stage_b(t):
            g = 0 if t < 4 else 1
            xb = XB[t % 3]
            kx = ("XB", t % 3)
            for hf in range(2):
                b = P.psum("x")
                for c4 in range(4):
                    c = hf * 4 + c4
                    P.transpose(ps(b, 128, c4 * 128, 128), V(xb[:, c * 128:(c + 1) * 128], kx), IDENT)
                for c4 in range(4):
                    c = hf * 4 + c4
                    dst = V(HT[:, c, t * 128:(t + 1) * 128], ("HT", t))
                    if hf == 0:
                        P.act(dst, ps(b, 128, c4 * 128, 128), AF.Identity, bias=V(ACBC[:, l, 1, c, g:g + 1], "ACBC"), scale=V(ACBC[:, l, 0, c, g:g + 1], "ACBC"))
                    else:
                        P.ts(dst, ps(b, 128, c4 * 128, 128), V(ACBC[:, l, 0, c, g:g + 1], "ACBC"), ALU.mult, V(ACBC[:, l, 1, c, g:g + 1], "ACBC"), ALU.add)
        stage_a(0)
        stage_a(1)
        for t in range(NT):
            stage_b(t)
            if t + 2 < NT:
                stage_a(t + 2)

    def proj2(k, prows_unused, col_lo, m, blk, bank=None):
        t0, n = BLOCKS[blk]
        b = P.psum("m") if bank is None else bank
        P.mm(ps(b, m, 0, n), [(wv(k, 128, c, col_lo, m), htv(c, t0, n)) for c in range(8)])
        return b

    def head_norm_rope(b, m, blk, gcol, out_bf, out_f32, tmp, rope, key_out, scale_extra=1.0):
        SQ, RS, QN, T1 = tmp
        onesm = V(CONS[0:m, 7, 0:m], "CONS")
        P.act(V(SQ[0:m, :], "SQ"), ps(b, m), AF.Square)
        b2 = P.psum("x")
        P.mm(ps(b2, m), [(onesm, V(SQ[0:m, :], "SQ"))])
        P.act(V(RS[0:m, :], "RS"), ps(b2, m), AF.Sqrt, bias=EPS, scale=1.0 / m)
        P.recip(V(RS[0:m, :], "RS"), V(RS[0:m, :], "RS"))
        if rope is None:
            if out_f32 is not None:
                P.stt(out_f32, ps(b, m), gcol, V(RS[0:m, :], "RS"), ALU.mult, ALU.mult)
                P.copy(out_bf, out_f32, eng="pool")
            else:
                P.stt(out_bf, ps(b, m), gcol, V(RS[0:m, :], "RS"), ALU.mult, ALU.mult)
            return
        perm, RT, s0 = rope
        P.stt(V(QN[0:m, :], "QN"), ps(b, m), gcol, V(RS[0:m, :], "RS"), ALU.mult, ALU.mult)
        b3 = P.psum("x")
        P.mm(ps(b3, m), [(perm, V(QN[0:m, :], "QN"))])
        P.tt(V(T1[0:m, :], "T1"), ps(b3, m), V(RT[0:m, 1, s0:s0 + 512], "ROPE"), ALU.mult)
        P.tt(V(QN[0:m, :], "QN"), V(QN[0:m, :], "QN"), V(RT[0:m, 0, s0:s0 + 512], "ROPE"), ALU.mult, eng="pool")
        P.tt(out_bf, V(QN[0:m, :], "QN"), V(T1[0:m, :], "T1"), ALU.add)

    def pair_norm_rope(b, gcol, out_bf, out_f32, tmp, s0):
        SQ, RS, QN, T1 = tmp
        P.act(V(SQ, "SQ"), ps(b), AF.Square)
        b2 = P.psum("x")
        P.mm(ps(b2), [(BDONES, V(SQ, "SQ"))])
        P.act(V(RS, "RS"), ps(b2), AF.Sqrt, bias=EPS, scale=1.0 / 64)
        P.recip(V(RS, "RS"), V(RS, "RS"))
        if s0 is None:
            if out_f32 is not None:
                P.stt(out_f32, ps(b), gcol, V(RS, "RS"), ALU.mult, ALU.mult)
                P.copy(out_bf, out_f32, eng="pool")
            else:
                P.stt(out_bf, ps(b), gcol, V(RS, "RS"), ALU.mult, ALU.mult)
            return
        P.stt(V(QN, "QN"), ps(b), gcol, V(RS, "RS"), ALU.mult, ALU.mult)
        b3 = P.psum("x")
        P.mm(ps(b3), [(BDPERM, V(QN, "QN"))])
        P.tt(V(T1, "T1"), ps(b3), V(R64[:, 1, s0:s0 + 512], "ROPE"), ALU.mult)
        P.tt(V(QN, "QN"), V(QN, "QN"), V(R64[:, 0, s0:s0 + 512], "ROPE"), ALU.mult, eng="pool")
        P.tt(out_bf, V(QN, "QN"), V(T1, "T1"), ALU.add)

    def staggered(gens):
        live = []
        it = iter(gens)
        while True:
            g_ = next(it, None)
            if g_ is not None:
                live.append(g_)
            elif not live:
                break
            nxt = []
            for g2 in live:
                try:
                    next(g2)
                    nxt.append(g2)
                except StopIteration:
                    pass
            live = nxt

    def pair_chain(proj_fn, gcol, out_bf, out_f32, tmp, s0, ti):
        SQ, RS, QN, T1 = tmp
        kS, kR, kQ, kT = ("SQ", ti), ("RS", ti), ("QN", ti), ("T1", ti)
        b = proj_fn()
        yield
        P.act(V(SQ, kS), ps(b), AF.Square)
        b2 = P.psum("x")
        P.mm(ps(b2), [(BDONES, V(SQ, kS))])
        yield
        P.act(V(RS, kR), ps(b2), AF.Sqrt, bias=EPS, scale=1.0 / 64)
        P.recip(V(RS, kR), V(RS, kR))
        if s0 is None:
            if out_f32 is not None:
                P.stt(out_f32, ps(b), gcol, V(RS, kR), ALU.mult, ALU.mult)
                P.copy(out_bf, out_f32, eng="pool")
            else:
                P.stt(out_bf, ps(b), gcol, V(RS, kR), ALU.mult, ALU.mult)
            return
        P.stt(V(QN, kQ), ps(b), gcol, V(RS, kR), ALU.mult, ALU.mult)
        b3 = P.psum("x")
        P.mm(ps(b3), [(BDPERM, V(QN, kQ))])
        yield
        P.tt(V(T1, kT), ps(b3), V(R64[:, 1, s0:s0 + 512], "ROPE"), ALU.mult)
        P.tt(V(QN, kQ), V(QN, kQ), V(R64[:, 0, s0:s0 + 512], "ROPE"), ALU.mult, eng="pool")
        P.tt(out_bf, V(QN, kQ), V(T1, kT), ALU.add)

    def norm_tmps():
        return [AR.alloc(nm, 128, [512]) for nm in ("SQ", "RS", "QN", "T1")]

    def phase_kv(l, QT, YA):
        tmp = norm_tmps()
        tmp_b = [AR.alloc(nm + "b", 128, [512]) for nm in ("SQ", "RS", "QN", "T1")]
        KSTG = AR.alloc("KSTG", 128, [512])
        KTS = AR.alloc("KTS", 128, [TS], BF16)
        TOK = [AR.alloc("TOK%d" % i, 128, [256]) for i in range(2)]
        VB = AR.alloc("VB", 128, [8, 128], BF16)
        CS = [AR.alloc("CS%d" % i, 128, [512]) for i in range(2)]
        CB = AR.alloc("CB", 128, [2, TS], BF16)
        CPB = AR.alloc("CPB", 128, [2, TP], BF16)
        KRS = AR.alloc("KRS", 32, [512])
        KRB = AR.alloc("KRB", 32, [T], BF16)
        agin = ag1_in[l].ap()
        kag = ("AG1", l)

        def job_ka_load():
            return load_w(w_in[l, :, O_KA:O_KA + 128], 128, 8, 128)

        def job_ka(k):
            for blk in range(3):
                b = proj2(k, 128, 0, 128, blk)
                gcol = small(2 * l + 1)
                if blk == 0:
                    pair_norm_rope(b, gcol, V(KTP, "KTP"), V(KSTG, "KSTG"), tmp, None)
                else:
                    s0 = (blk - 1) * 512
                    pair_norm_rope(b, gcol, V(KTS[:, s0:s0 + 512], "KTS"), None, tmp, s0)
            for t in range(4):
                bt = P.psum("x")
                P.transpose(ps(bt, 128, 0, 128), V(KSTG[:, t * 128:(t + 1) * 128], "KSTG"), IDENT)
                tk = TOK[t % 2]
                P.copy(V(tk[:, 0:128], ("TOK", t % 2)), ps(bt, 128, 0, 128))
                P.dma(V(o_k[t // 2, l, (t % 2) * 128:(t % 2 + 1) * 128, :], ("o.k", l, t)), V(tk[:, 0:128], ("TOK", t % 2)))
            P.dma(V(agin[0:128, :], kag), V(KTS, "KTS"))

        def job_va_load():
            return load_w(w_in[l, :, O_VA:O_VA + 128], 128, 8, 128)

        def job_va(k):
            for t in range(NT):
                b = P.psum("m")
                P.mm(ps(b, 128, 0, 128), [(htv(c, t * 128, 128), wv(k, 128, c, 0, 128)) for c in range(8)])
                if t < 4:
                    tk = TOK[t % 2]
                    P.copy(V(tk[:, 0:128], ("TOK", t % 2)), ps(b, 128, 0, 128))
                    P.dma(V(o_v[t // 2, l, (t % 2) * 128:(t % 2 + 1) * 128, :], ("o.v", l, t)), V(tk[:, 0:128], ("TOK", t % 2)))
                    P.copy(V(VAP[:, t, :, 0:64], "VAP"), V(PS[b][:, 0:128].rearrange("p (g d) -> p g d", g=2), ("PS", b)), eng="act")
                else:
                    P.copy(V(VB[:, t - 4, :], "VB"), ps(b, 128, 0, 128), eng="act")
            P.dma(V(agin[128:256, :].rearrange("r (a c) -> (r a) c", c=128).rearrange("(t p) c -> p t c", p=128), kag), V(VB, "VB"))

        def job_kv_load():
            return load_w(w_in[l, :, O_KV:O_KV + 288], 128, 8, 288)

        def job_kv(k):
            for blk in range(3):
                t0, n = BLOCKS[blk]
                bs = [proj2(k, 128, 128 * c2, 128, blk) for c2 in range(2)]
                SQ, RS, QN, T1 = tmp
                for c2 in range(2):
                    P.act(V((SQ if c2 == 0 else T1), ("SQ" if c2 == 0 else "T1")), ps(bs[c2]), AF.Square)
                b2 = P.psum("x")
                P.mm(ps(b2), [(ONES(), V(SQ, "SQ")), (ONES(), V(T1, "T1"))])
                P.act(V(RS, "RS"), ps(b2), AF.Sqrt, bias=EPS, scale=1.0 / 256)
                P.recip(V(RS, "RS"), V(RS, "RS"))
                for c2 in range(2):
                    gcol = small(12 + 2 * l + c2)
                    if blk == 0:
                        P.stt(V(CS[c2], ("CS", c2)), ps(bs[c2]), gcol, V(RS, "RS"), ALU.mult, ALU.mult)
                        P.copy(V(CPB[:, c2, :], "CPB"), V(CS[c2], ("CS", c2)), eng="pool")
                    else:
                        s0 = (blk - 1) * 512
                        P.stt(V(CB[:, c2, s0:s0 + 512], "CB"), ps(bs[c2]), gcol, V(RS, "RS"), ALU.mult, ALU.mult)
                if blk == 0:
                    for t in range(4):
                        bt = P.psum("x")
                        for c2 in range(2):
                            P.transpose(ps(bt, 128, 128 * c2, 128), V(CS[c2][:, t * 128:(t + 1) * 128], ("CS", c2)), IDENT)
                        tk = TOK[t % 2]
                        P.copy(V(tk, ("TOK", t % 2)), ps(bt, 128, 0, 256))
                        P.dma(V(o_ckv[t // 2, l, (t % 2) * 128:(t % 2 + 1) * 128, :], ("o.c", l, t)), V(tk, ("TOK", t % 2)))
                    P.dma(V(pkv[l, 0:256, :].rearrange("(c p) t -> p c t", p=128), ("PKV", l)), V(CPB, "CPB"))
                b = proj2(k, 128, 256, 32, blk)
                if blk == 0:
                    P.copy(V(KRS, "KRS"), ps(b, 32), eng="act")
                    P.copy(V(KRB[:, 0:512], "KRB"), V(KRS, "KRS"), eng="pool")
                    for t in range(4):
                        bt = P.psum("x")
                        P.transpose(ps(bt, 128, 0, 32), V(KRS[:, t * 128:(t + 1) * 128], "KRS"), V(CONS[0:32, 0, 0:32], "CONS"))
                        tk = TOK[t % 2]
                        P.copy(V(tk[:, 0:32], ("TOK", t % 2)), ps(bt, 128, 0, 32))
                        P.dma(V(o_kr[t // 2, l, (t % 2) * 128:(t % 2 + 1) * 128, :], ("o.r", l, t)), V(tk[:, 0:32], ("TOK", t % 2)))
                    P.dma(V(pkv[l, 256:288, :], ("PKV", l)), V(KRB[:, 0:512], "KRB"))
                else:
                    s0 = (blk - 1) * 512
                    SQ, RS, QN, T1 = tmp
                    P.copy(V(QN[0:32, :], "QN"), ps(b, 32), eng="act")
                    b3 = P.psum("x")
                    P.mm(ps(b3, 32), [(PERM32, V(QN[0:32, :], "QN"))])
                    P.tt(V(T1[0:32, :], "T1"), ps(b3, 32), V(R32[:, 1, s0:s0 + 512], "ROPE"), ALU.mult)
                    P.tt(V(QN[0:32, :], "QN"), V(QN[0:32, :], "QN"), V(R32[:, 0, s0:s0 + 512], "ROPE"), ALU.mult, eng="pool")
                    P.tt(V(KRB[:, 512 + s0:512 + s0 + 512], "KRB"), V(QN[0:32, :], "QN"), V(T1[0:32, :], "T1"), ALU.add)
            P.dma(V(agc_in[l].ap()[:, :].rearrange("(c p) t -> p c t", p=128), ("AGC", l)), V(CB, "CB"))
            P.dma(V(agin[256:288, :], kag), V(KRB[:, 512:T], "KRB"))

        extra = gqa_proj_jobs(l, QT, YA, [tmp, tmp_b])
        h_ka = job_ka_load()
        h_va = job_va_load()
        job_ka(h_ka)
        h_kv = job_kv_load()
        job_va(h_va)
        hx = [ld() for ld, _ in extra]
        job_kv(h_kv)
        if STAGE >= 1:
            grp = [[0, 1, 2, 3], [4, 5, 6, 7]]
            em.collective(cc_sems[3 * l],
                          lambda e: e.collective_compute("AllGather", ALU.bypass, replica_groups=grp,
                                                         ins=[ag1_in[l].ap().opt()], outs=[ag1_out[l].ap().opt()]),
                          reads=[kag], writes=[("AG1O", l)])
            em.collective(cc_sems[3 * l + 1],
                          lambda e: e.collective_compute("AllGather", ALU.bypass, replica_groups=grp,
                                                         ins=[agc_in[l].ap().opt()], outs=[agc_out[l].ap().opt()]),
                          reads=[("AGC", l)], writes=[("AGCO", l)])
        for (_, comp), h in zip(extra, hx):
            comp(h)


    mstate = {'first': True}

    def merge_branch(l, bi, Yv, nch, prow, alias=()):
        alias = list(alias)
        SG4 = AR.alloc("SG4", 128, [4, T], BF16)
        TB = [AR.alloc("TB%d" % i, 128, [512]) for i in range(2)]
        o_m = (O_M1, O_M2, O_M3)[bi]
        jobs = []
        for half in range(2):
            def load_m(half=half):
                return load_w(w_in[l, :, o_m + half * 512:o_m + (half + 1) * 512], 128, 8, 512)

            def comp_m(k, half=half):
                for c4 in range(4):
                    for blk in range(3):
                        t0, n = BLOCKS[blk]
                        b = proj2(k, 128, 128 * c4, 128, blk)
                        P.act(V(SG4[:, c4, t0:t0 + n], [("SG4", c4, blk)] + alias), ps(b), AF.Sigmoid)

            def load_o(half=half):
                return load_w(w_o[bi][l, :, half * 512:(half + 1) * 512], prow, nch, 512)

            def comp_o(k, half=half):
                for c4 in range(4):
                    oc = half * 4 + c4
                    for blk in range(3):
                        t0, n = BLOCKS[blk]
                        b = P.psum("m")
                        P.mm(ps(b), [(wv(k, prow, ch, c4 * 128, 128), Yv(ch, blk)) for ch in range(nch)])
                        mg = V(MERGED[:, oc, t0:t0 + n], ("MG", oc, blk))
                        sg = V(SG4[:, c4, t0:t0 + n], ("SG4", c4, blk))
                        if mstate['first']:
                            P.tt(mg, ps(b), sg, ALU.mult)
                        else:
                            tb = V(TB[(c4 + blk) % 2], ("TB", (c4 + blk) % 2))
                            P.tt(V(tb.ap, tb.keys + alias), ps(b), sg, ALU.mult)
                            P.tt(mg, mg, tb, ALU.add, eng="pool")
            jobs.append((load_m, comp_m))
            jobs.append((load_o, comp_o))
        run_jobs(jobs)
        mstate['first'] = False


    def attention_stream2(items, pt2, depth=2):
        dbl = [0, 1, 3]
        n = len(items)
        cur = {}
        for j in range(min(depth, n)):
            cur[j] = dbl[j % 3]
            items[j][0](cur[j])
        for i in range(n):
            if i + depth < n:
                cur[i + depth] = dbl[(i + depth) % 3]
                items[i + depth][0](cur[i + depth])
            d_ = cur.pop(i)
            ring = i % len(pt2)
            ptv = V(pt2[ring], ("PT", ring))
            P.act(ptv, V(PS2[d_][:, :], [("PS", 2 * d_), ("PS", 2 * d_ + 1)]), AF.Exp, scale=items[i][1])
            items[i][2](ring)
            if items[i][3] is not None:
                items[i][3]()

    def attention_stream(items, depth=2, nring=4):
        n = len(items)
        banks = {}
        for j in range(min(depth, n)):
            banks[j] = P.psum("m")
            items[j][0](banks[j])
        for i in range(n):
            if i + depth < n:
                banks[i + depth] = P.psum("m")
                items[i + depth][0](banks[i + depth])
            items[i][1](banks.pop(i), i % nring)
            items[i][2](i % nring)
            if items[i][3] is not None:
                items[i][3]()

    def gqa_proj_jobs(l, QT, YA, tmps):
        def mk(is_q):
            o = O_QA if is_q else O_GA

            def load():
                return load_w(w_in[l, :, o:o + 512], 128, 8, 512)

            def comp(k):
                def gate_chain(p, blk):
                    t0, n = BLOCKS[blk]
                    b = proj2(k, 128, 128 * p, 128, blk)
                    yield
                    P.act(V(YA[:, p, t0:t0 + n], ("YA", p, blk)), ps(b), AF.Silu)
                gens = []
                i = 0
                for p in range(4):
                    for blk in range(3):
                        t0, n = BLOCKS[blk]
                        if is_q:
                            gens.append(pair_chain(lambda p=p, blk=blk: proj2(k, 128, 128 * p, 128, blk), small(2 * l),
                                                   V(QT[:, p, t0:t0 + n], ("QT", p, blk)), None, tmps[i % 2],
                                                   None if blk == 0 else (blk - 1) * 512, i % 2))
                        else:
                            gens.append(gate_chain(p, blk))
                        i += 1
                staggered(gens)
            return load, comp
        return [mk(True), mk(False)]

    def phase_gqa(l, QT, YA, qt_off):
        off_a = AR.off
        KA = AR.alloc("KA", 128, [NKS], BF16)
        KB = AR.alloc("KB", 128, [NKS], BF16)
        VAS = AR.alloc("VAS", 128, [36, 128], BF16)
        KAP = AR.alloc("KAP", 128, [2, TP], BF16)
        KBP = AR.alloc("KBP", 128, [2, TP], BF16)
        PT2 = [AR.alloc("PT%d" % i, 128, [1024], BF16) for i in range(3)]
        PT = [PT2[i][:, 0:512] for i in range(3)]
        RC = AR.alloc("RC", 64, [512])
        TM = AR.alloc("TM", 64, [512])
        TM2 = AR.alloc("TM2", 128, [512])
        CK = [AR.alloc("CK%d" % i, 128, [64]) for i in range(2)]
        P.memset(V(VAS, "VAS"), 1.0)
        P.memset(V(KA, "KA"), 0.0)
        P.memset(V(KB, "KB"), 0.0)
        P.memset(V(KAP, "KAP"), 0.0)
        P.memset(V(KBP, "KBP"), 0.0)
        for g in range(2):
            src = V(KTP[64 * g:64 * g + 64, :], "KTP")
            P.copy(V(KAP[0:64, g, :], "KAP"), src, eng="act")
            P.copy(V(KBP[64:128, g, :], "KBP"), src, eng="act")

        def finish_head(ob, h, blk, t0, n):
            p, hh = divmod(h, 2)
            P.recip(V(RC[:, 0:n], "RC"), V(PS[ob][64:128, 0:n], ("PS", ob)))
            P.tt(V(TM[:, 0:n], "TM"), V(PS[ob][0:64, 0:n], ("PS", ob)), V(RC[:, 0:n], "RC"), ALU.mult)
            if hh == 0:
                ya = V(YA[0:64, p, t0:t0 + n], ("YA", p, blk))
                P.tt(ya, V(TM[:, 0:n], "TM"), ya, ALU.mult, eng="pool")
            else:
                P.copy(V(TM2[64:128, 0:n], "TM2"), V(TM[:, 0:n], "TM"), eng="pool")
                ya = V(YA[64:128, p, t0:t0 + n], ("YA", p, blk))
                P.tt(ya, V(TM2[64:128, 0:n], "TM2"), ya, ALU.mult, eng="pool")

        ago = ag1_out[l].ap()
        for g in range(2):
            for t in range(4):
                ck = V(CK[t % 2], ("CK", t % 2))
                P.dma(ck, V(c_k[l, t * 128:(t + 1) * 128, g * 64:(g + 1) * 64], "d.ck"))
                bt = P.psum("x")
                P.transpose(ps(bt, 64, 0, 128), ck, IDENT)
                P.copy(V(KA[0:64, t * 128:(t + 1) * 128], "KA"), ps(bt, 64, 0, 128))
                P.copy(V(KB[64:128, t * 128:(t + 1) * 128], "KB"), ps(bt, 64, 0, 128), eng="act")
            for r in range(4):
                srck = V(ago[r * 288 + g * 64:r * 288 + (g + 1) * 64, :], ("AG1O", l))
                P.dma(V(KA[0:64, PAST + r * TS:PAST + (r + 1) * TS], "KA"), srck)
                P.dma(V(KB[64:128, PAST + r * TS:PAST + (r + 1) * TS], "KB"), srck)
            P.dma(V(VAS[:, 0:4, 0:64], "VAS"), V(c_v[l, :, g * 64:(g + 1) * 64].rearrange("(t p) d -> p t d", p=128), "d.cv"), eng="pool")
            for r in range(4):
                src = ago[r * 288 + 128:r * 288 + 256, :].rearrange("r (a c) -> (r a) c", c=128)[:, g * 64:(g + 1) * 64].rearrange("(t p) d -> p t d", p=128)
                P.dma(V(VAS[:, 4 + 8 * r:4 + 8 * (r + 1), 0:64], "VAS"), V(src, ("AG1O", l)))
            items = []

            def add_head(h, blk, t0, n, kview, vview, ntile):
                ob = P.psum("a")
                qv = V(QT[:, h // 2, t0:t0 + n], ("QT", h // 2, blk))
                for kt in range(ntile):
                    def S(bank, kt=kt):
                        P.mm(ps(bank, 128, 0, n), [(kview(kt), qv)])

                    def E(bank, ring):
                        P.act(V(PT[ring][:, 0:n], ("PT", ring)), ps(bank, 128, 0, n), AF.Exp, scale=0.125)

                    def PV(ring, kt=kt):
                        P.mm1(ps(ob, 128, 0, n), vview(kt), V(PT[ring][:, 0:n], ("PT", ring)), start=(kt == 0), stop=(kt == ntile - 1),
                              inc=(kt == ntile - 1))
                    fin = (lambda: finish_head(ob, h, blk, t0, n)) if kt == ntile - 1 else None
                    items.append((S, E, PV, fin))
            for s_ in range(2):
                for hh in range(4):
                    h = g * 4 + hh
                    kp = KAP if h % 2 == 0 else KBP
                    add_head(h, 0, s_ * 256, 256,
                             lambda kt, s_=s_, kp=kp: V(kp[:, g, s_ * 256 + kt * 128:s_ * 256 + (kt + 1) * 128], "KAP"),
                             lambda kt, s_=s_: V(VAP[:, s_ * 2 + kt, g, :], "VAP"), 2)
            attention_stream(items, nring=3)
            items2 = []
            for qb in range(2):
                for hh in range(4):
                    h = g * 4 + hh
                    kk = KA if h % 2 == 0 else KB
                    ob = P.psum("a")
                    t0 = TP + qb * 512
                    qv = V(QT[:, h // 2, t0:t0 + 512], ("QT", h // 2, 1 + qb))
                    for pi in range(18):
                        def S(d_, pi=pi, kk=kk, qv=qv):
                            for j in range(2):
                                kt = 2 * pi + j
                                P.mm(ps(2 * d_ + j), [(V(kk[:, kt * 128:(kt + 1) * 128], "KA"), qv)])

                        def PV(ring, pi=pi, ob=ob):
                            for j in range(2):
                                kt = 2 * pi + j
                                P.mm1(ps(ob), V(VAS[:, kt, :], "VAS"), V(PT2[ring][:, j * 512:(j + 1) * 512], ("PT", ring)),
                                      start=(kt == 0), stop=(kt == 35), inc=(kt == 35))
                        fin = (lambda ob=ob, h=h, qb=qb, t0=t0: finish_head(ob, h, 1 + qb, t0, 512)) if pi == 17 else None
                        items2.append((S, 0.125, PV, fin))
            attention_stream2(items2, PT2)
        AR.off = qt_off
        merge_branch(l, 0, lambda ch, blk: V(YA[:, ch, BLOCKS[blk][0]:BLOCKS[blk][0] + BLOCKS[blk][1]], ("YA", ch, blk)), 4, 128,
                     alias=[("QT", p, blk) for p in range(4) for blk in range(3)])


    NKA = TP + NKS

    def rope_from(QNv_ap, m, perm, RT, s0, out_bf, tmp):
        SQ, RS, QN, T1 = tmp
        b3 = P.psum("x")
        P.mm(ps(b3, m), [(perm, V(QNv_ap, "QN"))])
        P.tt(V(T1[0:m, :], "T1"), ps(b3, m), V(RT[0:m, 1, s0:s0 + 512], "ROPE"), ALU.mult)
        P.tt(V(QNv_ap, "QN"), V(QNv_ap, "QN"), V(RT[0:m, 0, s0:s0 + 512], "ROPE"), ALU.mult, eng="pool")
        P.tt(out_bf, V(QNv_ap, "QN"), V(T1[0:m, :], "T1"), ALU.add)

    def phase_mla(l):
        YC = AR.alloc("YC", 128, [4, T], BF16)
        QC = AR.alloc("QC", 96, [4, T], BF16)
        KN = AR.alloc("KN", 96, [NKA], BF16)
        KR = KN[64:96, :]
        CKC = AR.alloc("CKC", 128, [2, 512], BF16)
        ONESB = AR.alloc("ONESB", 128, [128], BF16)
        P.memset(V(ONESB, "ONESB"), 1.0)
        off_a = AR.off
        QNT = AR.alloc("QNT", 128, [2, T], BF16)
        CKS = [AR.alloc("CKS%d" % i, 128, [384]) for i in range(2)]
        for i in range(2):
            P.memset(V(CKS[i], ("CKS", i)), 0.0)
        tmp = norm_tmps()
        SQ, RS, QN, T1 = tmp
        ago = ag1_out[l].ap()
        for t in range(4):
            cs = CKS[t % 2]
            kc = ("CKS", t % 2)
            P.dma(V(cs[:, 0:256], kc), V(c_ckv[l, t * 128:(t + 1) * 128, :], "d.cc"))
            P.dma(V(cs[:, 256:288], kc), V(c_kr[l, t * 128:(t + 1) * 128, :], "d.cr"))
            bt = P.psum("x")
            for c2 in range(2):
                P.transpose(ps(bt, 128, c2 * 128, 128), V(cs[:, c2 * 128:(c2 + 1) * 128], kc), IDENT)
            P.transpose(ps(bt, 128, 256, 128), V(cs[:, 256:384], kc), IDENT)
            P.copy(V(CKC[:, :, t * 128:(t + 1) * 128], "CKC"), V(PS[bt][:, 0:256].rearrange("p (c t) -> p c t", c=2), ("PS", bt)))
            P.copy(V(KR[:, TP + t * 128:TP + (t + 1) * 128], "KN"), ps(bt, 32, 256, 128), eng="act")
        P.dma(V(KR[:, 0:TP], "KN"), V(pkv[l, 256:288, :], ("PKV", l)))
        for r in range(4):
            P.dma(V(KR[:, TP + PAST + r * TS:TP + PAST + (r + 1) * TS], "KN"), V(ago[r * 288 + 256:r * 288 + 288, :], ("AG1O", l)))

        if MLA_CUT <= 1:
            return
        def load_ql():
            return load_w(w_in[l, :, O_QL:O_QL + 256], 128, 8, 256)

        def comp_ql(k):
            for blk in range(3):
                t0, n = BLOCKS[blk]
                bs = [proj2(k, 128, 128 * c2, 128, blk) for c2 in range(2)]
                for c2 in range(2):
                    P.act(V((SQ if c2 == 0 else T1), ("SQ" if c2 == 0 else "T1")), ps(bs[c2]), AF.Square)
                b2 = P.psum("x")
                P.mm(ps(b2), [(ONES(), V(SQ, "SQ")), (ONES(), V(T1, "T1"))])
                P.act(V(RS, "RS"), ps(b2), AF.Sqrt, bias=EPS, scale=1.0 / 256)
                P.recip(V(RS, "RS"), V(RS, "RS"))
                for c2 in range(2):
                    P.stt(V(QNT[:, c2, t0:t0 + n], ("QNT", blk)), ps(bs[c2]), small(8 + 2 * l + c2), V(RS, "RS"), ALU.mult, ALU.mult)

        def load_uq():
            return load_w(w_uq[l, :, :], 128, 2, 384)

        def comp_uq(k):
            for h in range(4):
                for blk in range(3):
                    t0, n = BLOCKS[blk]
                    b = P.psum("m")
                    P.mm(ps(b, 64), [(wv(k, 128, c2, h * 96, 64), V(QNT[:, c2, t0:t0 + n], ("QNT", blk))) for c2 in range(2)])
                    P.copy(V(QC[0:64, h, t0:t0 + n], ("QC", h, blk)), ps(b, 64), eng="act")
                    b = P.psum("m")
                    P.mm(ps(b, 32), [(wv(k, 128, c2, h * 96 + 64, 32), V(QNT[:, c2, t0:t0 + n], ("QNT", blk))) for c2 in range(2)])
                    dst = V(QC[64:96, h, t0:t0 + n], ("QC", h, blk))
                    if blk == 0:
                        P.copy(dst, ps(b, 32), eng="act")
                    else:
                        P.copy(V(QN[0:32, :], "QN"), ps(b, 32), eng="act")
                        rope_from(QN[0:32, :], 32, PERM32, R32, (blk - 1) * 512, V(SQ[0:32, :], "SQ"), tmp)
                        P.copy(dst, V(SQ[0:32, :], "SQ"), eng="act")

        def mk_gc(i):
            def load():
                return load_w(w_in[l, :, O_GC + i * 256:O_GC + (i + 1) * 256], 128, 8, 256)

            def comp(k):
                for cc in range(2):
                    ch = 2 * i + cc
                    for blk in range(3):
                        t0, n = BLOCKS[blk]
                        b = proj2(k, 128, 128 * cc, 128, blk)
                        P.act(V(YC[:, ch, t0:t0 + n], ("YC", ch, blk)), ps(b), AF.Silu)
            return load, comp
        jl = [(load_ql, comp_ql), (load_uq, comp_uq), mk_gc(0), mk_gc(1)]
        run_jobs(jl[:MLA_JOBS])

        if MLA_CUT <= 2:
            return
        VC = AR.alloc("VC", 128, [NKA // 128, 128], BF16)
        CST = [AR.alloc("CST%d" % i, 128, [2, 512], BF16) for i in range(2)]
        PT = [AR.alloc("PT%d" % i, 128, [512], BF16) for i in range(4)]
        RC = AR.alloc("RC", 128, [512])
        TM = AR.alloc("TM", 128, [512])
        PACC = [AR.alloc("PACC%d" % i, 128, [512]) for i in range(2)]
        hb_cnt = [0]
        kuk = load_w(w_ukv[l, :, :], 128, 2, 768)
        agc = agc_out[l].ap()
        sc = float(96.0 ** -0.5)

        def expand(h):
            for kb in range(10):
                if kb == 1:
                    cs, kc = CKC, "CKC"
                else:
                    cs, kc = CST[kb % 2], ("CST", kb % 2)
                    if kb == 0:
                        P.dma(V(cs, kc), V(pkv[l, 0:256, :].rearrange("(c p) t -> p c t", p=128), ("PKV", l)))
                    else:
                        r, hf = divmod(kb - 2, 2)
                        P.dma(V(cs, kc), V(agc[r * 256:(r + 1) * 256, hf * 512:(hf + 1) * 512].rearrange("(c p) t -> p c t", p=128), ("AGCO", l)))
                b = P.psum("m")
                P.mm(ps(b, 64), [(wv(kuk, 128, c2, h * 192, 64), V(cs[:, c2, :], kc)) for c2 in range(2)])
                P.copy(V(KN[0:64, kb * 512:(kb + 1) * 512], "KN"), ps(b, 64), eng="act")
                b = P.psum("m")
                for t4 in range(4):
                    P.mm(ps(b, 128, t4 * 128, 128), [(V(cs[:, c2, t4 * 128:(t4 + 1) * 128], kc), wv(kuk, 128, c2, h * 192 + 64, 128)) for c2 in range(2)])
                P.copy(V(VC[:, kb * 4:(kb + 1) * 4, :], "VC"), V(PS[b][:, :].rearrange("p (t d) -> p t d", t=4), ("PS", b)))

        def add_head(items, h, blk, t0, n, tiles):
            ob = P.psum("a")
            db = P.psum("x")
            q1 = V(QC[:, h, t0:t0 + n], ("QC", h, blk))
            last = len(tiles) - 1

            def fin():
                P.recip(V(RC[:, 0:n], "RC"), ps(db, 128, 0, n))
                P.tt(V(TM[:, 0:n], "TM"), ps(ob, 128, 0, n), V(RC[:, 0:n], "RC"), ALU.mult)
                yc = V(YC[:, h, t0:t0 + n], ("YC", h, blk))
                P.tt(yc, V(TM[:, 0:n], "TM"), yc, ALU.mult, eng="pool")
            for i, kt in enumerate(tiles):
                def S(bank, kt=kt):
                    P.mm1(ps(bank, 128, 0, n), V(KN[:, kt * 128:(kt + 1) * 128], "KN"), q1, True, True)

                def E(bank, ring):
                    P.act(V(PT[ring][:, 0:n], ("PT", ring)), ps(bank, 128, 0, n), AF.Exp, scale=sc)

                def PV(ring, i=i, kt=kt):
                    pt = V(PT[ring][:, 0:n], ("PT", ring))
                    P.mm1(ps(ob, 128, 0, n), V(VC[:, kt, :], "VC"), pt, i == 0, i == last, inc=False)
                    P.mm1(ps(db, 128, 0, n), V(ONESB, "ONESB"), pt, i == 0, i == last, inc=(i == last))
                items.append((S, E, PV, fin if i == last else None))

        for h in range(4):
            expand(h)
            items = []
            for s_ in range(2):
                add_head(items, h, 0, s_ * 256, 256, [2 * s_, 2 * s_ + 1])
            for qb in range(2):
                add_head(items, h, 1 + qb, TP + qb * 512, 512, list(range(4, NKA // 128)))
            attention_stream(items)
        if MLA_CUT <= 5:
            return
        merge_branch(l, 2, lambda ch, blk: V(YC[:, ch, BLOCKS[blk][0]:BLOCKS[blk][0] + BLOCKS[blk][1]], ("YC", ch, blk)), 4, 128)


    def phase_gla(l):
        YB = AR.alloc("YB", 128, [4, T], BF16)
        yb_off = AR.off
        QG = AR.alloc("QG", 128, [2, T], BF16)
        KGT = AR.alloc("KGT", 128, [2, T], BF16)
        KGK = AR.alloc("KGK", 128, [NT, 256], BF16)
        VGK = AR.alloc("VGK", 128, [NT, 512], BF16)
        RFA = AR.alloc("RFA", 49, [T])
        OGB = AR.alloc("OGB", 128, [4, T], BF16)
        WD = AR.alloc("WD", 49, [256])
        MASK4 = [AR.alloc("MASK4%d" % d, 128, [4, 128], BF16) for d in range(2)]
        S32 = [AR.alloc("S32%d" % d, 128, [2, 128]) for d in range(2)]
        SB16 = [AR.alloc("SB16%d" % d, 128, [2, 128], BF16) for d in range(2)]
        PTOT = AR.alloc("PTOT", 128, [2, 2])
        SPB = [AR.alloc("SPB%d" % i, 128, [256]) for i in range(2)]
        EXB = [AR.alloc("EX%d" % i, 128, [256]) for i in range(2)]
        KL = [AR.alloc("KL%d" % i, 128, [256], BF16) for i in range(2)]
        E1B = [AR.alloc("E1%d" % i, 128, [2, 128]) for i in range(2)]
        E2B = [AR.alloc("E2%d" % i, 128, [2, 128]) for i in range(2)]
        QDB = [AR.alloc("QD%d" % i, 128, [2, 128], BF16) for i in range(2)]
        KDB = [AR.alloc("KD%d" % i, 128, [2, 128], BF16) for i in range(2)]
        ATMB = [AR.alloc("ATM%d" % i, 128, [4, 128], BF16) for i in range(2)]
        DC = [AR.alloc("DC%d" % i, 128, [2]) for i in range(2)]
        OGFB = [AR.alloc("OGF%d" % i, 128, [512]) for i in range(2)]
        SQ2B = [AR.alloc("SQ2%d" % i, 128, [512]) for i in range(2)]
        RS2B = [AR.alloc("RS2%d" % i, 128, [512]) for i in range(2)]
        A2 = AR.alloc("A2", 128, [2])
        LM = AR.alloc("LM", 128, [256])
        FBP = [AR.alloc("FBP%d" % i, 128, [258]) for i in range(2)]
        vS = lambda d: V(S32[d], ("S32", d))
        vSB = lambda d: V(SB16[d], ("SB16", d))
        TRI = [V(CONS[:, 1, :], "CONS"), V(CONS[:, 3, :], "CONS")]
        TRIC = [V(CONS[:, 2, :], "CONS"), V(CONS[:, 4, :], "CONS")]
        NEG1 = V(CONS[:, 1, 127:128], "CONS")

        P.memset(V(RFA, "RFA"), 1.0)
        for d in range(2):
            P.dma(V(WD[32 * d:32 * d + 17, :], "WD"), V(wdec[l, d, :, :], "d.wd"))
            for h in range(4):
                P.copy(V(MASK4[d][:, h, :], ("MASK4", d)), V(CONS[:, 5 + d, :], "CONS"), eng="pool")

        def mk_qk(is_q):
            o = O_QG if is_q else O_KG

            def load():
                return load_w(w_in[l, :, o:o + 256], 128, 8, 256)

            def comp(k):
                for pr in range(2):
                    for blk in range(3):
                        t0, n = BLOCKS[blk]
                        b = proj2(k, 128, 128 * pr, 128, blk)
                        if is_q:
                            P.act(V(QG[:, pr, t0:t0 + n], ("QG", blk)), ps(b), AF.Identity, scale=0.125)
                        else:
                            P.copy(V(KGT[:, pr, t0:t0 + n], ("KGT", blk)), ps(b), eng="act")
                if not is_q:
                    for t in range(NT):
                        b = P.psum("m")
                        P.mm(ps(b, 128, 0, 256), [(htv(c, t * 128, 128), wv(k, 128, c, 0, 256)) for c in range(8)])
                        P.copy(V(KGK[:, t, :], ("KGK", t)), ps(b, 128, 0, 256))
            return load, comp

        def load_vg():
            return load_w(w_in[l, :, O_VG:O_VG + 512], 128, 8, 512)

        def comp_vg(k):
            for t in range(NT):
                b = P.psum("m")
                P.mm(ps(b), [(htv(c, t * 128, 128), wv(k, 128, c, 0, 512)) for c in range(8)])
                if t % 2 == 0:
                    P.copy(V(VGK[:, t, :], ("VGK", t)), ps(b))
                else:
                    P.copy(V(VGK[:, t, :], ("VGK", t)), ps(b), eng="act")

        def mk_gg(i):
            def load():
                return load_w(w_in[l, :, O_GG + i * 256:O_GG + (i + 1) * 256], 128, 8, 256)

            def comp(k):
                for cc in range(2):
                    ch = 2 * i + cc
                    for blk in range(3):
                        t0, n = BLOCKS[blk]
                        b = proj2(k, 128, 128 * cc, 128, blk)
                        P.act(V(YB[:, ch, t0:t0 + n], ("YB", ch, blk)), ps(b), AF.Silu)
            return load, comp

        def load_r():
            return load_w(w_in[l, :, O_RF:O_RF + 32], 128, 8, 32)

        def comp_r(k):
            for d in range(2):
                for blk in range(3):
                    t0, n = BLOCKS[blk]
                    b = proj2(k, 128, 16 * d, 16, blk)
                    P.copy(V(RFA[32 * d:32 * d + 16, t0:t0 + n], "RFA"), ps(b, 16), eng="act")
        run_jobs([(load_r, comp_r), mk_qk(False), (load_vg, comp_vg), mk_qk(True), mk_gg(0), mk_gg(1)])

        if GLA_CUT <= 1:
            return
        def prep(t, d, need_dc=True):
            i = d
            sp = V(SPB[i], ("SPB", i))
            ex = V(EXB[i], ("EX", i))
            bx = P.psum("x")
            P.mm(ps(bx, 128, 0, 256), [(V(RFA[32 * d:32 * d + 17, t * 128:(t + 1) * 128], "RFA"), V(WD[32 * d:32 * d + 17, :], "WD"))])
            yield
            P.act(ex, ps(bx, 128, 0, 256), AF.Exp, scale=-1.0)
            P.act(sp, ex, AF.Ln, bias=1.0)
            yield
            br = P.psum("x")
            P.mm(ps(br, 128, 0, 256), [(TRIC[d], sp)])
            yield
            P.act(ex, ps(br, 128, 0, 256), AF.Exp)
            kl = V(KL[i], ("KL", i))
            P.tt(kl, V(KGK[:, t, :], ("KGK", t)), ex, ALU.mult)
            yield
            if not need_dc:
                return sp, kl, None
            bt_ = P.psum("x")
            for pr in range(2):
                P.mm(ps(bt_, 128, pr, 1), [(V(SPB[i][:, pr * 128:(pr + 1) * 128], ("SPB", i)), NEG1)])
            yield
            dc = V(DC[i], ("DC", i))
            P.act(dc, ps(bt_, 128, 0, 2), AF.Exp)
            yield
            return sp, kl, dc

        def state_update(t, d, kl, dc):
            bs_ = P.psum("m")
            for h in range(4):
                pr = h // 2
                P.mm(ps(bs_, 128, h * 128, 128), [(V(kl.ap[:, pr * 128:(pr + 1) * 128], kl.keys[0]), V(VGK[:, t, h * 128:(h + 1) * 128], ("VGK", t)))])
            yield
            for h in range(4):
                pr, hh = divmod(h, 2)
                rows = slice(64 * hh, 64 * hh + 64)
                sv = V(S32[d][rows, pr, :], ("S32", d))
                P.stt(sv, sv, V(dc.ap[rows, pr:pr + 1], dc.keys[0]), V(PS[bs_][rows, h * 128:(h + 1) * 128], ("PS", bs_)), ALU.mult, ALU.add)
            P.copy(vSB(d), vS(d), eng="act")
            yield

        def step_state_only(t, d):
            sp, kl, dc = yield from prep(t, d)
            yield from state_update(t, d, kl, dc)
            pt_ = V(PTOT[:, d, :], "PTOT")
            P.tt(pt_, pt_, dc, ALU.mult)
            yield

        def step_full(t, d, second):
            blk = 0 if t < 4 else 1 + (t - 4) // 4
            E1, E2, QD, KD, ATM = E1B[d], E2B[d], QDB[d], KDB[d], ATMB[d]
            kE1, kE2, kQD, kKD, kATM = ("E1", d), ("E2", d), ("QD", d), ("KD", d), ("ATM", d)
            sp, kl, _ = yield from prep(t, d, need_dc=False)
            last = 127 if d == 0 else 0
            dc = V(E1[:, :, last], kE1)
            bc = P.psum("x")
            for pr in range(2):
                P.mm(ps(bc, 128, pr * 128, 128), [(V(sp.ap[:, pr * 128:(pr + 1) * 128], sp.keys[0]), TRI[d])])
            yield
            cv = V(PS[bc][:, 0:256].rearrange("p (a t) -> p a t", a=2), ("PS", bc))
            P.act(V(E1, kE1), cv, AF.Exp)
            P.act(V(E2, kE2), cv, AF.Exp, scale=-1.0)
            yield
            P.tt(V(QD, kQD), V(QG[:, :, t * 128:(t + 1) * 128], ("QG", blk)), V(E1, kE1), ALU.mult)
            P.tt(V(KD, kKD), V(KGT[:, :, t * 128:(t + 1) * 128], ("KGT", blk)), V(E2, kE2), ALU.mult)
            yield
            bas = [P.psum("m"), P.psum("m")]
            for h in range(4):
                pr, hh = divmod(h, 2)
                rows = slice(64 * hh, 64 * hh + 64)
                P.mm(ps(bas[hh], 128, pr * 128, 128), [(V(KD[rows, pr, :], kKD), V(QD[rows, pr, :], kQD))])
            yield
            atm4 = ATM[:, :, :].rearrange("p (a b) t -> p a b t", b=2)
            msk4 = MASK4[d][:, 0:2, :]
            for hh in range(2):
                P.tt(V(atm4[:, :, hh, :], kATM), V(PS[bas[hh]][:, 0:256].rearrange("p (a t) -> p a t", a=2), ("PS", bas[hh])), V(msk4, ("MASK4", d)), ALU.mult)
            yield
            bo = P.psum("a")
            for h in range(4):
                pr, hh = divmod(h, 2)
                rows = slice(64 * hh, 64 * hh + 64)
                P.mm1(ps(bo, 128, h * 128, 128), V(VGK[:, t, h * 128:(h + 1) * 128], ("VGK", t)), V(ATM[:, h, :], kATM), True, False, inc=False)
                P.mm1(ps(bo, 128, h * 128, 128), V(SB16[d][rows, pr, :], ("SB16", d)), V(QD[rows, pr, :], kQD), False, True)
            yield
            ogb = V(OGB[:, :, t * 128:(t + 1) * 128], ("OGB", t))
            pso = V(PS[bo][:, :].rearrange("p (h t) -> p h t", h=4), ("PS", bo))
            if not second:
                P.copy(ogb, pso, eng="act")
                yield
            else:
                OGF, SQ2, RS2 = OGFB[d], SQ2B[d], RS2B[d]
                kO, kS, kR = ("OGF", d), ("SQ2", d), ("RS2", d)
                ogf = V(OGF[:, :].rearrange("p (h t) -> p h t", h=4), kO)
                P.tt(ogf, pso, ogb, ALU.add)
                yield
                P.tt(V(SQ2, kS), V(OGF, kO), V(OGF, kO), ALU.mult)
                yield
                bn = P.psum("x")
                P.mm(ps(bn), [(ONES(), V(SQ2, kS))])
                yield
                P.act(V(RS2, kR), ps(bn), AF.Ln, bias=EPS, scale=1.0 / 128)
                P.act(V(RS2, kR), V(RS2, kR), AF.Exp, scale=-0.5)
                yield
                P.stt(V(SQ2, kS), V(OGF, kO), small(4 + l), V(RS2, kR), ALU.mult, ALU.mult)
                yb = V(YB[:, :, t * 128:(t + 1) * 128], [("YB", ch, blk) for ch in range(4)])
                P.tt(yb, V(SQ2[:, :].rearrange("p (h t) -> p h t", h=4), kS), yb, ALU.mult, eng="pool")
                yield
            yield from state_update(t, d, kl, dc)

        def run_chains(chains):
            live = list(chains)
            while live:
                nxt = []
                for g_ in live:
                    try:
                        next(g_)
                        nxt.append(g_)
                    except StopIteration:
                        pass
                live = nxt

        def chain(tiles, d, full):
            n = len(tiles)
            for k in range(n):
                t = tiles[k] if d == 0 else tiles[n - 1 - k]
                if full:
                    yield from step_full(t, d, k >= n - 1 - k and not (k == n - 1 - k and d == 1))
                else:
                    yield from step_state_only(t, d)

        def zero_state(d):
            P.memset(vS(d), 0.0)
            P.memset(vSB(d), 0.0)

        SAMPLE = list(range(4, NT))
        P.memset(V(PTOT, "PTOT"), 1.0)
        agi = ag2_in[l].ap()
        for d in range(2):
            zero_state(d)
        run_chains([chain(SAMPLE, 0, False), chain(SAMPLE, 1, False)])
        for d in range(2):
            P.dma(V(agi[d * 128:(d + 1) * 128, 0:256], ("AG2", l)), V(S32[d][:, :, :].rearrange("p a v -> p (a v)"), ("S32", d)))
            P.dma(V(agi[d * 128:(d + 1) * 128, 256:258], ("AG2", l)), V(PTOT[:, d, :], "PTOT"))
        em.collective(cc_sems[3 * l + 2],
                      lambda e: e.collective_compute("AllGather", ALU.bypass, replica_groups=[[0, 1, 2, 3], [4, 5, 6, 7]],
                                                     ins=[ag2_in[l].ap().opt()], outs=[ag2_out[l].ap().opt()]),
                      reads=[("AG2", l)], writes=[("AG2O", l)])
        st_view = lambda o, sq: o[sq, l].rearrange("(pr hh) k v -> (hh k) pr v", hh=2)
        for sq in range(2):
            tiles = [2 * sq, 2 * sq + 1]
            zero_state(0)
            zero_state(1)
            run_chains([chain(tiles, 0, True), chain(tiles, 1, True)])
            P.dma(V(st_view(o_sb, sq), ("o.sb", l, sq)), vS(1))
            P.dma(V(st_view(o_sf, sq), ("o.sf", l, sq)), vS(0))
        ago2 = ag2_out[l].ap()
        fcnt = 0
        for d in range(2):
            src = (s_f, s_b)[d]
            P.dma(vS(d), V(src[l].rearrange("(pr hh) k v -> (hh k) pr v", hh=2), "d.st"))
            for r in (range(4) if d == 0 else range(3, -1, -1)):
                m_c = V(FOLD[:, d, r, 0:1], "FOLD")
                om_c = V(FOLD[:, d, r, 1:2], "FOLD")
                fb = FBP[fcnt % 2]
                kfb = ("FBP", fcnt % 2)
                fcnt += 1
                P.dma(V(fb, kfb), V(ago2[r * 256 + d * 128:r * 256 + (d + 1) * 128, :], ("AG2O", l)))
                P.ts(V(A2, "A2"), V(fb[:, 256:258], kfb), m_c, ALU.mult, om_c, ALU.add)
                P.ts(V(LM, "LM"), V(fb[:, 0:256], kfb), m_c, ALU.mult)
                for pr in range(2):
                    sv = V(S32[d][:, pr, :], ("S32", d))
                    P.stt(sv, sv, V(A2[:, pr:pr + 1], "A2"), V(LM[:, pr * 128:(pr + 1) * 128], "LM"), ALU.mult, ALU.add)
            P.copy(vSB(d), vS(d), eng="act")
        run_chains([chain(SAMPLE, 0, True), chain(SAMPLE, 1, True)])
        AR.off = yb_off
        merge_branch(l, 1, lambda ch, blk: V(YB[:, ch, BLOCKS[blk][0]:BLOCKS[blk][0] + BLOCKS[blk][1]], ("YB", ch, blk)), 4, 128,
                     alias=[("QG", blk) for blk in range(3)] + [("KGT", blk) for blk in range(3)] + [("KGK", t) for t in range(NT)])


    def phase_out(l):
        src = xin if l == 0 else xs
        dst = xs if l == 0 else y
        XT = [AR.alloc("XT%d" % i, 128, [D]) for i in range(3)]
        O32 = [AR.alloc("O32%d" % i, 128, [D]) for i in range(3)]
        JK = AR.alloc("JK", 128, [512], BF16)
        ST = AR.alloc("STO", 128, [NT, 4])
        P.memset(V(ST, "STO"), 0.0)
        kh = [load_w(w_out[l, :, hf * 512:(hf + 1) * 512], 128, 8, 512) for hf in range(2)]
        for t in range(NT):
            g = 0 if t < 4 else 1
            i = t % 3
            xt = V(XT[i], ("XT", i))
            o32 = V(O32[i], ("O32", i))
            P.dma(xt, V(src[t * 128:(t + 1) * 128, :], ("xs", t)))
            for hf in range(2):
                b = P.psum("m")
                P.mm(ps(b), [(V(MERGED[:, c, t * 128:(t + 1) * 128], "MERGED"), wv(kh[hf], 128, c, 0, 512)) for c in range(8)])
                oh = V(O32[i][:, hf * 512:(hf + 1) * 512], ("O32", i))
                P.copy(oh, ps(b), eng=("act" if hf == 0 else "dve"))
                P.act(V(JK, "JK"), oh, AF.Square, accum=V(ST[:, t, hf:hf + 1], "STO"))
            P.tt(V(ST[:, t, 2:3], "STO"), V(ST[:, t, 0:1], "STO"), V(ST[:, t, 1:2], "STO"), ALU.add)
            P.act(V(ST[:, t, 3:4], "STO"), V(ST[:, t, 2:3], "STO"), AF.Sqrt, bias=EPS, scale=1.0 / D)
            P.recip(V(ST[:, t, 3:4], "STO"), V(ST[:, t, 3:4], "STO"))
            P.stt(o32, o32, V(ST[:, t, 3:4], "STO"), V(GB[:, l, g, :], "GB"), ALU.mult, ALU.mult)
            P.tt(o32, o32, xt, ALU.add)
            P.dma(V(dst[t * 128:(t + 1) * 128, :], ("xs", t) if l == 0 else ("y", t)), o32)

    KTP = None
    VAP = None

    phase_mod()
    for l in range(L):
        mstate['first'] = True
        new_phase()
        phase_ht(l)
        new_phase()
        KTP = AR.alloc("KTP", 128, [TP], BF16)
        VAP = AR.alloc("VAP", 128, [4, 2, 128], BF16)
        P.memset(V(VAP, "VAP"), 1.0)
        qt_off = AR.off
        QT = AR.alloc("QT", 128, [4, T], BF16)
        YA = AR.alloc("YA", 128, [4, T], BF16)
        layer_base = AR.off
        phase_kv(l, QT, YA)
        if STAGE <= 1:
            break
        new_phase(layer_base)
        if not SKIP_GQA:
            phase_gqa(l, QT, YA, qt_off)
        if STAGE <= 2:
            em.barrier()
            P.em.dma("sp", dbg[:, :, :], MERGED[:], reads=[], writes=["dbg"])
            break
        new_phase(0)
        if not SKIP_MLA:
            phase_mla(l)
        if STAGE <= 3:
            em.barrier()
            P.em.dma("sp", dbg[:, :, :], MERGED[:], reads=[], writes=["dbg"])
            break
        new_phase(0)
        phase_gla(l)
        if STAGE <= 4:
            em.barrier()
            P.em.dma("sp", dbg[:, :, :], MERGED[:], reads=[], writes=["dbg"])
            break
        new_phase(0)
        phase_out(l)
        if STAGE <= 5:
            break
    em.barrier()
    em.finish("sp")
    em.replay(block)
    P.st.close()
    return nc


_CACHE = {}


def _consts():
    c = np.zeros((128, 12, 128), np.float32)
    i = np.arange(128)
    c[:, 0, :] = np.eye(128)
    s = -1.0 / 16.0
    c[:, 1, :] = s * (i[:, None] <= i[None, :])
    c[:, 2, :] = s * (i[:, None] > i[None, :])
    c[:, 3, :] = s * (i[:, None] >= i[None, :])
    c[:, 4, :] = s * (i[:, None] < i[None, :])
    c[:, 5, :] = (i[:, None] <= i[None, :])
    c[:, 6, :] = (i[:, None] >= i[None, :])
    c[:, 7, :] = 1.0
    c[0:64, 8, 0:64] = perm_matrix(64)
    c[0:32, 9, 0:32] = perm_matrix(32)
    c[0:64, 10, 0:64] = 1.0
    c[64:128, 10, 64:128] = 1.0
    c[0:64, 11, 0:64] = perm_matrix(64)
    c[64:128, 11, 64:128] = perm_matrix(64)
    return c


def kernel(**inp):
    f = lambda a: np.ascontiguousarray(np.asarray(a, dtype=np.float32))
    x_prompt, x_sample = f(inp["x_prompt"]), f(inp["x_sample"])
    if "nc" not in _CACHE:
        _CACHE["nc"] = build_program()
    nc = _CACHE["nc"]
    col = lambda v, p: np.ascontiguousarray(v.reshape(v.shape[0], -1, p).transpose(2, 0, 1))
    wdec = np.stack([np.concatenate([f(inp["w_gla_decay_fwd"]), f(inp["b_gla_decay_fwd"])[:, None, :]], 1),
                     np.concatenate([f(inp["w_gla_decay_bwd"]), f(inp["b_gla_decay_bwd"])[:, None, :]], 1)], 1)
    shared = {
        "b_mod": f(inp["b_mod"]), "g_post": f(inp["g_post"]),
        "g_pre_c": col(f(inp["g_pre"]), 128), "w_in": f(inp["w_in"]),
        "gqk_c": np.ascontiguousarray(np.stack([f(inp["g_q_norm"]), f(inp["g_k_norm"])], -1).transpose(1, 0, 2)),
        "wdec": np.ascontiguousarray(wdec),
        "g_gla_c": np.ascontiguousarray(f(inp["g_gla_out"]).T),
        "g_mq_c": col(f(inp["g_mla_q"]), 128), "g_mkv_c": col(f(inp["g_mla_kv"]), 128),
        "w_uq": f(inp["w_mla_uq"]), "w_ukv": f(inp["w_mla_ukv"]),
        "w_o_gqa": f(inp["w_o_gqa"]), "w_o_gla": f(inp["w_o_gla"]), "w_o_mla": f(inp["w_o_mla"]),
        "w_out": f(inp["w_out"]), "consts": _consts(),
    }
    c, c_ctx = f(inp["c"]), f(inp["c_ctx"])
    w_mod_full = f(inp["w_mod"])
    condT3 = np.ascontiguousarray(np.stack([c_ctx, c[0], c[1]], 0).reshape(3, 8, 128).transpose(2, 1, 0))
    in_maps = []
    for core in range(NCORES):
        b, q = divmod(core, 4)
        m = dict(shared)
        m["xin"] = np.concatenate([x_prompt[2 * core].reshape(256, D), x_prompt[2 * core + 1].reshape(256, D),
                                   x_sample[b, q * TS:(q + 1) * TS]], 0)
        m["condT"] = condT3
        m["w_mod"] = np.ascontiguousarray(w_mod_full[:, :, q * 768:(q + 1) * 768])
        sel = np.zeros((4, 2 + 256), np.float32)
        sel[0, 0] = sel[1 + b, 1] = sel[3, 0] = sel[3, 1] = 1.0
        sel[0, 2:130] = 1.0
        sel[1 + b, 130:258] = 1.0
        sel[3, 2:258] = 1.0
        m["sel3"] = sel
        m["c_k"] = f(inp["cache_gqa_k"])[b].reshape(L, PAST, 128)
        m["c_v"] = f(inp["cache_gqa_v"])[b].reshape(L, PAST, 128)
        m["c_ckv"] = f(inp["cache_mla_ckv"])[b]
        m["c_kr"] = f(inp["cache_mla_krope"])[b]
        m["s_f"] = f(inp["state_gla_fwd"])[b]
        m["s_b"] = f(inp["state_gla_bwd"])[b]
        c64, s64, c32, s32 = rope_tables(core)
        m["rope64"] = np.ascontiguousarray(np.stack([c64, s64], 1))
        m["rope32"] = np.ascontiguousarray(np.stack([c32, s32], 1))
        fm = np.zeros((128, 2, 4, 2), np.float32)
        for r in range(4):
            fm[:, 0, r, 0] = 1.0 if r < q else 0.0
            fm[:, 1, r, 0] = 1.0 if r > q else 0.0
        fm[..., 1] = 1.0 - fm[..., 0]
        m["foldm"] = fm
        in_maps.append(m)
    res = run_bass_kernel_spmd(nc, in_maps, core_ids=list(range(NCORES)))
    R = res.results
    y_prompt = np.stack([R[i // 2]["y"][(i % 2) * 256:(i % 2 + 1) * 256] for i in range(16)], 0)
    y_sample = np.stack([np.concatenate([R[b * 4 + q]["y"][TP:] for q in range(4)], 0) for b in range(2)], 0)
    cat = lambda k: np.concatenate([R[i][k] for i in range(NCORES)], 0)
    return (y_prompt.astype(np.float32), y_sample.astype(np.float32),
            cat("o_k").reshape(16, L, 256, 2, 64), cat("o_v").reshape(16, L, 256, 2, 64),
            cat("o_ckv"), cat("o_kr"), cat("o_sf"), cat("o_sb"))
```
